# Optimizing a Trainium2 kernel written in Bass

```python
import math
import jax, jax.numpy as jnp
from jax import lax
import numpy as np

D_MODEL = 1024
BATCH = 8
SEQ = 4096
DEPTH = 1

N_META = 16
D_MIX = D_MODEL
HEAD_DIM = 64
D_FOX = D_MIX // 2
D_RWKV = D_MIX - D_FOX
H_FOX = D_FOX // HEAD_DIM
H_RWKV = D_RWKV // HEAD_DIM
RANK_W = 64
RANK_A = 64
Q_BLOCK = 128
NORM_EPS = 1e-6
GN_EPS = 64e-5
NEG_INF = -1e30

FOX_SIZES = (D_FOX, D_FOX, D_FOX, H_FOX, D_FOX)
RWKV_SIZES = (D_RWKV, D_RWKV, D_RWKV, RANK_W, RANK_A, D_RWKV)
D_FOX_IN = sum(FOX_SIZES)
D_RWKV_IN = sum(RWKV_SIZES)
D_IN = D_FOX_IN + D_RWKV_IN
D_SHIFT = 3 * D_RWKV + RANK_W + RANK_A

kernel_name = "hymba_fox_rwkv7_hybrid_block"


def _split(p, sizes):
    offs = [int(o) for o in np.cumsum(sizes)[:-1]]
    return jnp.split(p, offs, axis=-1)


def rmsnorm(x, w):
    xf = x.astype(jnp.float32)
    y = xf * lax.rsqrt(jnp.mean(xf * xf, axis=-1, keepdims=True) + NORM_EPS)
    return (y * w.astype(jnp.float32)).astype(x.dtype)


def _heads(t, n_heads):
    b, l, _ = t.shape
    return t.reshape(b, l, n_heads, -1).transpose(0, 2, 1, 3)


def _fox_block(q, k, v, cq, ck, q_start):
    qb, kb = q.shape[2], k.shape[2]
    s = jnp.einsum('bhqd,bhkd->bhqk', q, k).astype(jnp.float32) / math.sqrt(HEAD_DIM)
    s = s + cq[..., :, None] - ck[..., None, :]
    q_pos = q_start + jnp.arange(qb)
    k_pos = jnp.arange(kb)
    s = jnp.where(k_pos[None, :] <= q_pos[:, None], s, NEG_INF)
    p = jax.nn.softmax(s, axis=-1)
    return jnp.einsum('bhqk,bhkd->bhqd', p, v.astype(jnp.float32))


def fox_branch(p, b_f):
    b, l, _ = p.shape
    q, k, v, fl, z = _split(p, FOX_SIZES)
    q, k, v = _heads(q, H_FOX), _heads(k, H_FOX), _heads(v, H_FOX)
    log_f = jax.nn.log_sigmoid(fl.astype(jnp.float32) + b_f.astype(jnp.float32))
    c = jnp.cumsum(log_f, axis=1).transpose(0, 2, 1)
    n_real = l - N_META
    bounds = [(0, N_META)] + [(N_META + i * Q_BLOCK, min(N_META + (i + 1) * Q_BLOCK, l))
                              for i in range(-(-n_real // Q_BLOCK))]
    outs = [_fox_block(q[:, :, s0:s1], k[:, :, :s1], v[:, :, :s1], c[:, :, s0:s1], c[:, :, :s1], s0)
            for (s0, s1) in bounds]
    o = jnp.concatenate(outs, axis=2).transpose(0, 2, 1, 3).reshape(b, l, D_FOX)
    return (o * jax.nn.silu(z.astype(jnp.float32))).astype(p.dtype)


def _rwkv7_step(S, inp):
    r_t, w_t, k_t, v_t, a_t, b_t = inp
    sa = jnp.einsum('bhij,bhj->bhi', S, a_t)
    S = S * w_t[:, :, None, :] + sa[..., None] * b_t[:, :, None, :] + v_t[..., None] * k_t[:, :, None, :]
    y = jnp.einsum('bhij,bhj->bhi', S, r_t)
    return S, y


def rwkv_branch(p, mu, w0, w_up, a0, a_up, k_k, k_a, r_k, gn_w, gn_b):
    b, l, _ = p.shape
    f32 = jnp.float32
    ps, z = p[..., :D_SHIFT], p[..., D_SHIFT:]
    prev = jnp.pad(ps, ((0, 0), (1, 0), (0, 0)))[:, :-1]
    ps = (ps + mu * (prev - ps)).astype(f32)
    r, k, v, wd, ad = _split(ps, (D_RWKV, D_RWKV, D_RWKV, RANK_W, RANK_A))
    w = -jax.nn.softplus(-(w0.astype(f32) + jnp.tanh(wd) @ w_up.astype(f32))) - 0.5
    decay = jnp.exp(-jnp.exp(w))
    a = jax.nn.sigmoid(a0.astype(f32) + ad @ a_up.astype(f32))
    kk = (k * k_k.astype(f32)).reshape(b, l, H_RWKV, HEAD_DIM)
    kk = kk * lax.rsqrt(jnp.sum(kk * kk, axis=-1, keepdims=True) + 1e-12)
    k = k * (1.0 + (a - 1.0) * k_a.astype(f32))
    hd = lambda t: t.reshape(b, l, H_RWKV, HEAD_DIM)
    r, k, v, decay, a = hd(r), hd(k), hd(v), hd(decay), hd(a)
    a_vec, b_vec = -kk, kk * a
    tm = lambda t: jnp.moveaxis(t, 1, 0)
    S0 = jnp.zeros((b, H_RWKV, HEAD_DIM, HEAD_DIM), f32)
    _, y = lax.scan(_rwkv7_step, S0, (tm(r), tm(decay), tm(k), tm(v), tm(a_vec), tm(b_vec)))
    y = jnp.moveaxis(y, 0, 1)
    mean = jnp.mean(y, axis=-1, keepdims=True)
    var = jnp.mean(jnp.square(y - mean), axis=-1, keepdims=True)
    y = ((y - mean) * lax.rsqrt(var + GN_EPS)).reshape(b, l, D_RWKV)
    y = y * gn_w.astype(f32) + gn_b.astype(f32)
    bonus = jnp.sum(r * k * r_k.astype(f32), axis=-1, keepdims=True) * v
    y = y + bonus.reshape(b, l, D_RWKV)
    return (y * jax.nn.silu(z.astype(f32))).astype(p.dtype)


def setup_inputs(seed: int = 0) -> dict:
    key = jax.random.key(seed)
    ks = jax.random.split(key, 20)
    nrm = jax.random.normal
    x = nrm(ks[0], (BATCH, SEQ, D_MODEL), jnp.float32)
    meta = nrm(ks[1], (N_META, D_MODEL), jnp.float32)
    norm_w = 1.0 + 0.1 * nrm(ks[2], (DEPTH, D_MODEL), jnp.float32)
    w_in = nrm(ks[3], (DEPTH, D_MODEL, D_IN), jnp.float32) * D_MODEL ** -0.5
    b_f = jax.random.uniform(ks[4], (DEPTH, H_FOX), jnp.float32, 1.0, 5.0)
    mu_shift = jax.random.uniform(ks[5], (DEPTH, D_SHIFT), jnp.float32)
    w0 = jax.random.uniform(ks[6], (DEPTH, D_RWKV), jnp.float32, -7.0, -1.0)
    w_up = 0.5 * nrm(ks[7], (DEPTH, RANK_W, D_RWKV), jnp.float32) * RANK_W ** -0.5
    a0 = 0.5 * nrm(ks[8], (DEPTH, D_RWKV), jnp.float32)
    a_up = 0.5 * nrm(ks[9], (DEPTH, RANK_A, D_RWKV), jnp.float32) * RANK_A ** -0.5
    k_k = 0.85 + 0.05 * nrm(ks[10], (DEPTH, D_RWKV), jnp.float32)
    k_a = 1.0 + 0.05 * nrm(ks[11], (DEPTH, D_RWKV), jnp.float32)
    r_k = 0.1 * nrm(ks[12], (DEPTH, H_RWKV, HEAD_DIM), jnp.float32)
    gn_w = 1.0 + 0.1 * nrm(ks[13], (DEPTH, D_RWKV), jnp.float32)
    gn_b = 0.02 * nrm(ks[14], (DEPTH, D_RWKV), jnp.float32)
    w_out = nrm(ks[15], (DEPTH, D_MIX, D_MODEL), jnp.float32) * D_MIX ** -0.5
    final_norm_w = 1.0 + 0.1 * nrm(ks[16], (D_MODEL,), jnp.float32)
    return {"x": x, "meta": meta, "norm_w": norm_w, "w_in": w_in, "b_f": b_f,
            "mu_shift": mu_shift, "w0": w0, "w_up": w_up, "a0": a0, "a_up": a_up,
            "k_k": k_k, "k_a": k_a, "r_k": r_k, "gn_w": gn_w, "gn_b": gn_b,
            "w_out": w_out, "final_norm_w": final_norm_w}


def reference(x, meta, norm_w, w_in, b_f, mu_shift, w0, w_up, a0, a_up, k_k, k_a, r_k,
              gn_w, gn_b, w_out, final_norm_w):
    b = x.shape[0]
    h = jnp.concatenate([jnp.broadcast_to(meta[None].astype(x.dtype), (b, N_META, x.shape[-1])), x], axis=1)
    for l in range(DEPTH):
        u = rmsnorm(h, norm_w[l])
        p = u @ w_in[l]
        p_fox, p_rwkv = p[..., :D_FOX_IN], p[..., D_FOX_IN:]
        y_fox = fox_branch(p_fox, b_f[l])
        y_rwkv = rwkv_branch(p_rwkv, mu_shift[l], w0[l], w_up[l], a0[l], a_up[l],
                             k_k[l], k_a[l], r_k[l], gn_w[l], gn_b[l])
        h = h + jnp.concatenate([y_fox, y_rwkv], axis=-1) @ w_out[l]
    h = rmsnorm(h, final_norm_w)
    return h[:, N_META:]
```

```python
import contextlib
import math
import threading
import numpy as np
import concourse.bass as bass
import concourse.mybir as mybir
from concourse.bass_utils import run_bass_kernel_spmd

F32 = mybir.dt.float32
BF16 = mybir.dt.bfloat16
AF = mybir.ActivationFunctionType
ALU = mybir.AluOpType
AX = mybir.AxisListType

ENGS = ("pe", "act", "dve", "pool", "sp")
NDMASEM = 24
LAM = math.exp(-0.5)
NORM_EPS = 1e-6
GN_EPS = 64e-5


class _Op:
    __slots__ = ("eng", "fn", "deps", "needs_inc", "val", "is_dma", "dma_slot", "dma_val", "sid")

    def __init__(self, eng, fn):
        self.eng, self.fn = eng, fn
        self.deps = []
        self.needs_inc = False
        self.val = None
        self.is_dma = False
        self.dma_slot = None
        self.dma_val = None
        self.sid = None


class Prog:
    def __init__(self, nc):
        self.nc = nc
        self.ops = {e: [] for e in ENGS}
        self.last_w = {}
        self.readers = {}
        self.ndma = 0
        self.dma_last = [None] * NDMASEM
        self.final_waits = []
        self.hook = None
        self.pe_ok = set()
        self._tl = threading.local()
        self._npair = 0

    def run_pair(self, fa, fb, qa=1, qb=1):
        cv = threading.Condition()
        st = {"turn": 0, "done": [False, False], "cnt": 0, "err": None}
        quota = [qa, qb]
        self.quota = quota
        tl = threading.local()
        self._pair = (cv, st, quota, tl)

        def hook():
            me = tl.idx
            with cv:
                st["cnt"] += 1
                if st["cnt"] >= quota[me] and not st["done"][1 - me]:
                    st["cnt"] = 0
                    st["turn"] = 1 - me
                    cv.notify_all()
                    while st["turn"] != me:
                        cv.wait()

        self._npair += 1
        npair = self._npair

        def runner(idx, f):
            tl.idx = idx
            self._tl.sid = (npair, idx)
            with cv:
                while st["turn"] != idx:
                    cv.wait()
            try:
                f()
            except BaseException as ex:
                st["err"] = ex
            finally:
                with cv:
                    st["done"][idx] = True
                    st["cnt"] = 0
                    st["turn"] = 1 - idx
                    cv.notify_all()

        self.hook = hook
        ta = threading.Thread(target=runner, args=(0, fa))
        tb = threading.Thread(target=runner, args=(1, fb))
        ta.start(); tb.start(); ta.join(); tb.join()
        self.hook = None
        self._pair = None
        if st["err"] is not None:
            raise st["err"]

    def give(self, k):
        if getattr(self, "_pair", None) is None or k <= 0:
            return
        cv, st, quota, tl = self._pair
        me = tl.idx
        with cv:
            if st["done"][1 - me]:
                return
            quota[1 - me] = k
            st["cnt"] = 0
            st["turn"] = 1 - me
            cv.notify_all()
            while st["turn"] != me:
                cv.wait()

    def set_q(self, qa, qb):
        if getattr(self, "quota", None) is not None:
            self.quota[0], self.quota[1] = qa, qb

    def _dep(self, op, d):
        if d is None or d is op:
            return
        if op.eng == "pe" and d.eng == "pe" and not d.is_dma:
            return
        if d not in op.deps:
            op.deps.append(d)
        d.needs_inc = True

    def op(self, eng, fn, reads=(), writes=(), pe_start=False):
        o = _Op(eng, fn)
        o.sid = getattr(self._tl, "sid", None)
        self.ops[eng].append(o)
        if eng == "pe" and pe_start:
            for w in writes:
                lw = self.last_w.get(w)
                if (lw is not None and lw.eng == "pe" and not lw.is_dma and not self.readers.get(w)
                        and lw.sid != getattr(self._tl, "sid", None)):
                    raise RuntimeError("PSUM resource %r: new matmul group over unread PE data" % (w,))
        for r in reads:
            self._dep(o, self.last_w.get(r))
        for w in writes:
            self._dep(o, self.last_w.get(w))
            for rd in self.readers.get(w, ()):
                self._dep(o, rd)
        for r in reads:
            self.readers.setdefault(r, []).append(o)
        for w in writes:
            self.last_w[w] = o
            self.readers[w] = []
        if self.hook is not None:
            self.hook()
        return o

    def dma(self, eng, fn, reads=(), writes=(), final=False):
        o = self.op(eng, fn, reads, writes)
        o.is_dma = True
        slot = self.ndma % NDMASEM
        o.dma_slot = slot
        o.dma_val = 16 * (self.ndma // NDMASEM + 1)
        prev = self.dma_last[slot]
        if prev is not None and prev not in o.deps:
            o.deps.append(prev)
        self.dma_last[slot] = o
        self.ndma += 1
        if final:
            self.final_waits.append(o)
        return o

    def barrier(self):
        lasts = [self.ops[e][-1] for e in ENGS if self.ops[e]]
        lasts += [d for d in self.dma_last if d is not None]
        for e in ENGS:
            o = _Op(e, None)
            self.ops[e].append(o)
            for d in lasts:
                if d is o:
                    continue
                if d not in o.deps:
                    o.deps.append(d)
                if not d.is_dma:
                    d.needs_inc = True
        self.last_w = {}
        self.readers = {}

    def emit(self):
        nc = self.nc
        with contextlib.ExitStack() as st:
            sems = {e: st.enter_context(nc.semaphore("s_" + e)) for e in ENGS}
            dsems = [st.enter_context(nc.semaphore("d_%d" % i)) for i in range(NDMASEM)]
            for e in ENGS:
                c = 0
                for o in self.ops[e]:
                    if o.is_dma or o.fn is None:
                        continue
                    if o.needs_inc:
                        c += 1
                        o.val = c
            block = st.enter_context(nc.Block())
            engobj = {"pe": "tensor", "act": "scalar", "dve": "vector", "pool": "gpsimd", "sp": "sync"}

            def make(e):
                def body(engine):
                    waited = {}
                    for o in self.ops[e]:
                        for d in o.deps:
                            if d.is_dma:
                                key, sem, v = ("d", d.dma_slot), dsems[d.dma_slot], d.dma_val
                            else:
                                key, sem, v = d.eng, sems[d.eng], d.val
                            if waited.get(key, 0) >= v:
                                continue
                            engine.wait_ge(sem, v)
                            waited[key] = v
                        if o.fn is None:
                            continue
                        ins = o.fn(engine)
                        if o.is_dma:
                            ins.then_inc(dsems[o.dma_slot], 16)
                        elif o.needs_inc:
                            ins.then_inc(sems[e], 1)
                    if e == "sp":
                        for d in self.final_waits:
                            engine.wait_ge(dsems[d.dma_slot], d.dma_val)
                return body

            for e in ENGS:
                getattr(block, engobj[e])(make(e))


D = 1024
NHEAD = 8
HD = 64
C_FOX = 2056
C_RWKV = 2176


class _Stop(Exception):
    pass


Q1A, Q1B, Q2A, Q2B = 4, 1, 2, 1
BSW = 17
GV_A, GV_X, GV_L, GV_S, GV_Y, GV_T, GV_TA = 16, 6, 4, 2, 6, 2, 1


def build(nblk_real=32, debug=False, stage=99):
    NB = nblk_real + 1
    LTOT = 16 + 128 * nblk_real
    nc = bass.Bass("TRN2", target_bir_lowering=False)
    dt_in = lambda n, s: nc.dram_tensor(n, s, F32, kind="ExternalInput").ap()
    x = dt_in("x", [128 * nblk_real, D])
    meta = dt_in("meta", [16, D])
    norm_w = dt_in("norm_w", [1, D])
    w_in = dt_in("w_in", [D, 4232])
    b_f = dt_in("b_f", [1, 8])
    mu_shift = dt_in("mu_shift", [1, 1664])
    w0 = dt_in("w0", [1, 512])
    w_up = dt_in("w_up", [64, 512])
    a0 = dt_in("a0", [1, 512])
    a_up = dt_in("a_up", [64, 512])
    k_k = dt_in("k_k", [1, 512])
    k_a = dt_in("k_a", [1, 512])
    r_k = dt_in("r_k", [1, 512])
    gn_w = dt_in("gn_w", [1, 512])
    gn_b = dt_in("gn_b", [1, 512])
    w_out = dt_in("w_out", [D, D])
    fin_w = dt_in("final_norm_w", [1, D])
    out = nc.dram_tensor("out", [128 * nblk_real, D], F32, kind="ExternalOutput").ap()
    dbg = None
    if debug:
        dbg = nc.dram_tensor("dbg", [128 * nblk_real, D], F32, kind="ExternalOutput").ap()

    P = Prog(nc)

    def mm(o, lhsT, rhs, start, stop, rd, wr):
        P.op("pe", lambda e: e.matmul(o, lhsT=lhsT, rhs=rhs, start=start, stop=stop, skip_group_check=True), rd, wr, pe_start=start)

    def act(o, i, func, rd, wr, bias=None, scale=None, accum=None):
        kw = {}
        if bias is not None:
            kw["bias"] = bias
        if scale is not None:
            kw["scale"] = scale
        if accum is not None:
            kw["accum_out"] = accum
        P.op("act", lambda e: e.activation(out=o, in_=i, func=func, **kw), rd, wr)

    def tt(eng, o, a, b, op, rd, wr):
        P.op(eng, lambda e: e.tensor_tensor(out=o, in0=a, in1=b, op=op), rd, wr)

    def ts(eng, o, a, s1, s2, op0, op1, rd, wr):
        if s2 is None:
            P.op(eng, lambda e: e.tensor_scalar(out=o, in0=a, scalar1=s1, scalar2=None, op0=op0), rd, wr)
        else:
            P.op(eng, lambda e: e.tensor_scalar(out=o, in0=a, scalar1=s1, scalar2=s2, op0=op0, op1=op1), rd, wr)

    def stt(o, a, s, b, op0, op1, rd, wr):
        P.op("dve", lambda e: e.scalar_tensor_tensor(out=o, in0=a, scalar=s, in1=b, op0=op0, op1=op1), rd, wr)

    def cp(eng, o, i, rd, wr):
        if eng == "act":
            P.op("act", lambda e: e.activation(out=o, in_=i, func=AF.Copy), rd, wr)
        else:
            P.op(eng, lambda e: e.tensor_copy(out=o, in_=i), rd, wr)

    def memset(eng, o, v, wr):
        P.op(eng, lambda e: e.memset(o, v), (), wr)

    def asel(o, pattern, cmp, cm, base, name):
        P.op("pool", lambda e: e.affine_select(out=o, in_=o, pattern=pattern, compare_op=cmp, fill=0.0,
                                               base=base, channel_multiplier=cm), [name], [name])

    def dma(o, i, rd, wr, final=False, slow=False):
        if slow:
            P.dma("sp", lambda e: e.dma_start(out=o, in_=i, allow_slow_non_contiguous=True), rd, wr, final=final)
        else:
            P.dma("sp", lambda e: e.dma_start(out=o, in_=i), rd, wr, final=final)

    def blk_T(b):
        return 16 if b == 0 else 128

    def blk_t0(b):
        return 0 if b == 0 else 16 + 128 * (b - 1)

    G = contextlib.ExitStack()
    if True:
        SB = lambda n, s, d: G.enter_context(nc.sbuf_tensor(n, s, d))
        PS = lambda n, s, d: G.enter_context(nc.psum_tensor(n, s, d))
        W = SB("W", [128, 8, C_RWKV], BF16)
        stg = SB("stg", [128, 1, 1088], F32)
        yfox = SB("yfox", [128, nblk_real, 512], BF16)
        ident = SB("ident", [128, 128], BF16)
        identf = SB("identf", [128, 128], F32)
        mincl_f = SB("mincl_f", [128, 128], F32)
        ones_f = SB("ones_f", [128, 128], F32)
        normw_col = SB("normw_col", [128, 8], F32)
        xt = SB("xt", [128, 2, D], F32)
        xs = SB("xs", [128, D], BF16)
        uT = SB("uT", [128, 8, 128], BF16)
        st_ = SB("stats", [128, 16], F32)
        pP0 = PS("pP0", [128, 512], F32)
        pT = PS("pT", [128, 1024], BF16)

        memset("pool", identf[:], 1.0, ["identf"])
        asel(identf[:], [[-1, 128]], ALU.is_equal, 1, 0, "identf")
        cp("dve", ident[:], identf[:], ["identf"], ["ident"])
        memset("pool", mincl_f[:], 1.0, ["mincl_f"])
        asel(mincl_f[:], [[1, 128]], ALU.is_ge, -1, 0, "mincl_f")
        memset("pool", ones_f[:], 1.0, ["ones_f"])
        rowsT = SB("rowsT", [48, 128], F32)
        dma(rowsT[0:8, :], norm_w[0, :].rearrange("(c p) -> c p", p=128), [], ["rowsT"])
        mm(pP0[:, 0:8], rowsT[0:8, :], identf[0:8, 0:8], True, True, ["rowsT", "identf"], ["pP0"])
        cp("dve", normw_col[:, 0:8], pP0[:, 0:8], ["pP0"], ["normw_col"])

        wq = [0]
        stg_slots = [(stg[:, 0, :], "stg0")]
        wout = SB("wout", [128, 8, D], BF16)

        def load_w(dst_fn, src_fn, ncols, nparts=128):
            i = wq[0]
            wq[0] += 1
            sap, sname = stg_slots[i % len(stg_slots)]
            dma(sap[0:nparts, 0:ncols], src_fn, ["_"], [sname])
            eng = ("act", "dve", "pool")[i % 3]
            cp(eng, dst_fn, sap[0:nparts, 0:ncols], [sname], ["W"])

        def xsl(slot):
            return xt[:, slot, :] if slot < 2 else stg[:, 0, 0:D]

        def front(b, slot):
            T = blk_T(b)
            src = meta[:, :] if b == 0 else x[(b - 1) * 128:b * 128, :]
            xr = "xt%d" % slot
            dma(xsl(slot)[0:T, :], src, ["stg0"] if slot == 2 else [], [xr, "stg0"] if slot == 2 else [xr])
            act(xs[0:T, :], xsl(slot)[0:T, :], AF.Square, [xr], ["xs", "ss"], accum=st_[0:T, 0:1])
            act(st_[0:T, 1:2], st_[0:T, 0:1], AF.Ln, ["ss"], ["sd"], bias=NORM_EPS, scale=1.0 / D)
            act(st_[0:T, 2:3], st_[0:T, 1:2], AF.Exp, ["sd"], ["rstd"], scale=-0.5)
            act(xs[0:T, :], xsl(slot)[0:T, :], AF.Copy, [xr, "rstd"], ["xs"], scale=st_[0:T, 2:3])
            for dc in range(8):
                P.op("pe", lambda e, dc=dc: e.transpose(out=pT[:, dc * 128:dc * 128 + T],
                                                        in_=xs[0:T, dc * 128:(dc + 1) * 128],
                                                        identity=ident[0:T, 0:T]), ["xs", "ident"], ["pT"])
            pTv = pT[:, :].rearrange("p (c t) -> p c t", c=8)
            tt("dve", uT[:, :, 0:T], pTv[:, :, 0:T], normw_col[:, 0:8].unsqueeze(2).to_broadcast([128, 8, T]),
               ALU.mult, ["pT", "normw_col"], ["uT"])

        def proj_fm(pb, name, col0, nch, T):
            pv = pb[:, :].rearrange("p (c t) -> p c t", c=4)
            for c in range(nch):
                for dc in range(8):
                    mm(pv[:, c, 0:T], W[:, dc, col0 + c * 128:col0 + (c + 1) * 128], uT[:, dc, 0:T],
                       dc == 0, dc == 7, ["W", "uT"], [name])
            return pv

        def proj_tm(pb, name, col0, ncol, T):
            for dc in range(8):
                mm(pb[0:T, 0:ncol], uT[:, dc, 0:T], W[:, dc, col0:col0 + ncol], dc == 0, dc == 7, ["W", "uT"], [name])

        with contextlib.ExitStack() as L1:
            S1 = lambda n, s, d: L1.enter_context(nc.sbuf_tensor(n, s, d))
            KT = S1("KT", [128, 4, LTOT], BF16)
            Vsb = S1("Vsb", [128, NB, 8, 65], BF16)
            extK = S1("extK", [128, LTOT], BF16)
            QT = S1("QT", [128, 2, 4, 2, 128], BF16)
            Rext = S1("Rext", [128, 2, 8, 128], BF16)
            hmask = S1("hmask", [48, 8, 128], BF16)
            PT = S1("PT", [128, 2, 8, 128], BF16)
            mincl_b = S1("mincl_b", [128, 128], BF16)
            ZY = S1("ZY", [128, 2, 512], F32)
            zf = ZY
            yft = S1("yft", [128, 512], F32)
            hmaskf = ZY[0:48, :, :].rearrange("p a (h t) -> p (a h) t", h=4)
            PK = S1("PK", [128, 8, 6], BF16)
            PQ = S1("PQ", [128, 8, 6], BF16)
            sm = S1("sm", [128, 64], F32)
            Aacc = S1("Aacc", [128, 8], F32)
            bf_bc = S1("bf_bc", [128, 8], F32)
            zeros_b = S1("zeros_b", [128, 512], BF16)
            stg1 = S1("stg1", [128, 2, 1088], F32)
            for k_ in range(2):
                stg_slots.append((stg1[:, k_, :], "stg%d" % (k_ + 1)))
            pS = L1.enter_context(nc.psum_tensor("pS", [128, 2048], F32))
            pO = L1.enter_context(nc.psum_tensor("pO", [128, 1024], F32))

            for dc in range(8):
                for hf in range(2):
                    load_w(W[:, dc, hf * 1028:(hf + 1) * 1028], w_in[dc * 128:(dc + 1) * 128, hf * 1028:(hf + 1) * 1028], 1028)
            cp("dve", mincl_b[:], mincl_f[:], ["mincl_f"], ["mincl_b"])
            memset("pool", hmaskf, 1.0, ["ZY"])
            asel(hmaskf, [[-6, 8], [0, 128]], ALU.is_ge, 1, 0, "ZY")
            asel(hmaskf, [[6, 8], [0, 128]], ALU.is_ge, -1, 5, "ZY")
            cp("dve", hmask[:], hmaskf, ["ZY"], ["hmask"])
            memset("pool", Vsb[:, :, :, 64:65], 1.0, ["Vones"])
            memset("pool", PK[:], 1.0, ["PK"])
            memset("pool", PQ[:], 1.0, ["PQ"])
            memset("pool", Aacc[:], 0.0, ["Aacc"])
            memset("pool", zeros_b[:], 0.0, ["zeros_b"])
            memset("pool", extK[:], 0.0, ["eK_init"])
            memset("pool", Rext[:], 0.0, ["Rext_init"])
            memset("pool", QT[:], 0.0, ["QT_init"])
            dma(bf_bc[:], b_f[0:1, :].to_broadcast([128, 8]), [], ["bf_bc"])

            pSv2 = pS[:, :].rearrange("p (s h q) -> p s h q", s=2, h=8)
            pOv = pO[:, :].rearrange("p (h c) -> p h c", h=8)
            def pre1(b):
                T = blk_T(b)
                t0 = blk_t0(b)
                par = b % 2
                two = (b - 1) <= BSW
                pPx, pPxn = (pS[:, 1536:2048], "pS11") if two else (pP0, "pP0")
                front(b, b % 2)
                yield
                if b > 0:
                    pv = proj_fm(pP0, "pP0", 0, 4, T)
                    act(QT[0:64, par, :, 0, 0:T], pv[0:64, :, 0:T], AF.Copy, ["pP0", "QT_init"], ["QT%d" % par], scale=0.125)
                    act(QT[64:128, par, :, 1, 0:T], pv[64:128, :, 0:T], AF.Copy, ["pP0", "QT_init"], ["QT%d" % par], scale=0.125)
                    yield
                pv = proj_fm(pPx, pPxn, 512, 4, T)
                cp("dve", KT[:, :, t0:t0 + T], pv[:, :, 0:T], [pPxn], ["KT%d" % b])
                yield
                proj_tm(pP0, "pP0", 1024, 512, T)
                cp("act", Vsb[0:T, b, :, 0:64], pP0[0:T, :].rearrange("p (h c) -> p h c", h=8), ["pP0"], ["V%d" % b])
                proj_tm(pPx, pPxn, 1536, 8, T)
                tt("dve", sm[0:T, 0:8], pPx[0:T, 0:8], bf_bc[0:T, :], ALU.add, [pPxn, "bf_bc"], ["xf"])
                yield
                if b > 0:
                    proj_tm(pP0, "pP0", 1544, 512, T)
                    act(zf[0:T, par, :], pP0[0:T, :], AF.Silu, ["pP0"], ["zf%d" % par, "ZY"])
                    yield
                act(sm[0:T, 8:16], sm[0:T, 0:8], AF.Exp, ["xf"], ["e1"], scale=-1.0)
                act(sm[0:T, 16:24], sm[0:T, 8:16], AF.Ln, ["e1"], ["lfn"], bias=1.0)
                mm(pPx[0:T, 16:24], mincl_f[0:T, 0:T], sm[0:T, 16:24], True, False, ["lfn", "mincl_f"], [pPxn])
                mm(pPx[0:T, 16:24], ones_f[:, 0:T], Aacc[:, :], False, True, ["Aacc", "ones_f"], [pPxn])
                cp("act", sm[0:T, 24:32], pPx[0:T, 16:24], [pPxn], ["cpos"])
                tt("dve", Aacc[0:T, :], Aacc[0:T, :], sm[0:T, 16:24], ALU.add, ["Aacc", "lfn"], ["Aacc"])
                cp("dve", PK[0:T, :, 0], sm[0:T, 24:32], ["cpos"], ["PK"])
                tt("dve", sm[0:T, 32:40], sm[0:T, 24:32], PK[0:T, :, 0], ALU.subtract, ["cpos", "PK"], ["r1"])
                cp("dve", PK[0:T, :, 1], sm[0:T, 32:40], ["r1"], ["PK"])
                tt("dve", sm[0:T, 40:48], sm[0:T, 32:40], PK[0:T, :, 1], ALU.subtract, ["r1", "PK"], ["r2"])
                cp("dve", PK[0:T, :, 2], sm[0:T, 40:48], ["r2"], ["PK"])
                yield
                P.op("pe", lambda e, T=T: e.transpose(out=pT[0:48, 0:T], in_=PK[0:T, :, :].rearrange("p h i -> p (h i)"),
                                                      identity=ident[0:T, 0:T]), ["PK", "ident"], ["pT"])
                cp("act", extK[0:48, t0:t0 + T], pT[0:48, 0:T], ["pT", "eK_init"], ["eK%d" % b])
                if b == 0:
                    return
                ts("dve", PQ[0:T, :, 3:6], PK[0:T, :, 0:3], -1.0, None, ALU.mult, None, ["PK"], ["PQ"])
                P.op("pe", lambda e, T=T: e.transpose(out=pT[0:48, 128:128 + T], in_=PQ[0:T, :, :].rearrange("p h i -> p (h i)"),
                                                      identity=ident[0:T, 0:T]), ["PQ", "ident"], ["pT"])
                tt("dve", Rext[0:48, par, :, 0:T], pT[0:48, 128:128 + T].unsqueeze(1).to_broadcast([48, 8, T]), hmask[:, :, 0:T],
                   ALU.mult, ["pT", "hmask", "Rext_init"], ["Rext%d" % par])
                yield

            def main1(b):
                if b == 0:
                    return
                T = blk_T(b)
                par = b % 2
                for hf in range(2):
                    mm(pO[0:T, hf * 512:(hf + 1) * 512], zeros_b[:, 0:T], zeros_b[:, 0:512], True, False,
                       ["zeros_b"], ["pO"])
                def s_part(kb):
                    Tk = blk_T(kb)
                    k0 = blk_t0(kb)
                    sl = kb % 2
                    ptn = "PT%d" % sl
                    ss_ = sl if b > BSW else 0
                    pSv = pSv2[:, ss_]
                    for hf in range(2):
                        psn = "pS%d%d" % (ss_, hf)
                        mm(pSv[0:Tk, hf * 4:(hf + 1) * 4, 0:T], extK[:, k0:k0 + Tk], Rext[:, par, hf * 4:(hf + 1) * 4, 0:T],
                           True, False, ["eK%d" % kb, "Rext%d" % par, "Rext_init", "eK_init"], [psn])
                        for hh in range(4):
                            h = hf * 4 + hh
                            g = h // 2
                            mm(pSv[0:Tk, h, 0:T], KT[:, g, k0:k0 + Tk], QT[:, par, g, h % 2, 0:T],
                               False, hh == 3, ["KT%d" % kb, "QT%d" % par, "QT_init"], [psn])
                        act(PT[0:Tk, sl, hf * 4:(hf + 1) * 4, 0:T], pSv[0:Tk, hf * 4:(hf + 1) * 4, 0:T], AF.Exp, [psn], [ptn])
                    if kb == b:
                        tt("dve", PT[0:Tk, sl, :, 0:T], PT[0:Tk, sl, :, 0:T],
                           mincl_b[0:Tk, 0:T].unsqueeze(1).to_broadcast([Tk, 8, T]), ALU.mult, [ptn, "mincl_b"], [ptn])

                def pv_part(kb):
                    Tk = blk_T(kb)
                    sl = kb % 2
                    ptn = "PT%d" % sl
                    for h in range(8):
                        mm(pOv[0:T, h, 0:65], PT[0:Tk, sl, h, 0:T], Vsb[0:Tk, kb, h, 0:65], False, kb == b,
                           [ptn, "V%d" % kb, "Vones"], ["pO"])

                s_part(0)
                for kb in range(b + 1):
                    if kb + 1 <= b:
                        s_part(kb + 1)
                    pv_part(kb)
                    yield
                P.op("dve", lambda e, T=T: e.reciprocal(out=sm[0:T, 48:56], in_=pOv[0:T, :, 64]), ["pO"], ["rden"])
                tt("dve", yft[0:T, :].rearrange("p (h c) -> p h c", h=8), pOv[0:T, :, 0:64],
                   sm[0:T, 48:56].unsqueeze(2).to_broadcast([T, 8, 64]), ALU.mult, ["pO", "rden"], ["yft"])
                tt("dve", yfox[0:T, b - 1, :], yft[0:T, :], zf[0:T, par, :], ALU.mult, ["yft", "zf%d" % par], ["yfox"])
                yield

            def interleave(gm, gp):
                dm = dp = False
                while not (dm and dp):
                    if not dm:
                        try:
                            next(gm)
                        except StopIteration:
                            dm = True
                    if not dp:
                        try:
                            next(gp)
                        except StopIteration:
                            dp = True

            if stage > 1:
                for _ in pre1(0):
                    pass
                drain = lambda g: (lambda: [None for _ in g])
                def load_w2():
                    for dc in range(8):
                        for hf in range(2):
                            load_w(W[:, dc, hf * 1088:(hf + 1) * 1088],
                                   w_in[dc * 128:(dc + 1) * 128, C_FOX + hf * 1088:C_FOX + (hf + 1) * 1088], 1088)
                    for dc in range(8):
                        load_w(wout[:, dc, :], w_out[dc * 128:(dc + 1) * 128, :], 1024)

                for b in range(NB):
                    if b + 1 < NB:
                        P.run_pair(drain(main1(b)), drain(pre1(b + 1)), Q1A, Q1B)
                    else:
                        P.run_pair(drain(main1(b)), load_w2, 12, 1)

        del stg_slots[1:]
        P.barrier()

        with contextlib.ExitStack() as L2:
          if stage >= 3:
            S2 = lambda n, s, d: L2.enter_context(nc.sbuf_tensor(n, s, d))
            WA = S2("WA", [128, 2, 512], BF16)
            PF = S2("PF", [128, 13, 129], F32)
            Dl = S2("Dl", [128, 13, 128], F32)
            f32t = lambda n: S2(n, [128, 4, 128], F32)
            SG, AVt, CUM, EXC = f32t("SG"), f32t("AVt"), f32t("CUM"), f32t("EXC")
            GM, GP, GPREV, GHAT = f32t("GM"), f32t("GP"), f32t("GPREV"), f32t("GHAT")
            SQ, KK, F1, KP = f32t("SQ"), f32t("KK"), f32t("F1"), f32t("KP")
            KKS, RN, RK, BB = EXC, SQ, F1, SG
            LR = S2("LR", [128, 128], BF16)
            AR = S2("AR", [128, 2, 4, 2, 2, 128], BF16)
            Kt = S2("Kt", [128, 2, 4, 128], BF16)
            Bt = S2("Bt", [128, 2, 4, 128], BF16)
            Kh = S2("Kh", [128, 4, 128], BF16)
            Bh = S2("Bh", [128, 4, 128], BF16)
            vb = S2("vb", [128, 4, 128], BF16)
            VT = S2("VT", [128, 2, 512], BF16)
            KhT = S2("KhT", [128, 2, 512], BF16)
            BhT = S2("BhT", [128, 2, 512], BF16)
            MK = S2("MK", [128, 8, 4, 128], BF16)
            MbT = S2("MbT", [128, 8, 128], BF16)
            PPb = S2("PPb", [128, 8, 2, 2, 128], BF16)
            Xb = S2("Xb", [128, 8, 2, 64], BF16)
            STf = S2("STf", [128, 4, 64], F32)
            STb = S2("STb", [128, 4, 64], BF16)
            YS = S2("YS", [128, 1024], F32)
            Ysb = YS[:, 0:512]
            sqy = YS[:, 512:1024]
            Hh = YS
            zr = S2("zr", [128, 2, 512], F32)
            sbv = S2("sbv", [128, 2, 8], F32)
            gamC = S2("gamC", [128, 2, 4], F32)
            mixr = S2("mixr", [128, 512], BF16)
            mixT = S2("mixT", [128, 8, 128], BF16)
            mask4 = S2("mask4", [128, 4, 128], F32)
            mstT = S2("mstT", [128, 128], F32)
            blk = S2("blk", [128, 128], F32)
            E2 = S2("E2", [128, 2], F32)
            finw_bc = S2("finw_bc", [128, D], F32)
            gnw_bc = S2("gnw_bc", [128, 512], F32)
            gnb_bc = S2("gnb_bc", [128, 512], F32)
            cols = S2("cols", [128, 48], F32)
            g8 = S2("g8", [128, 64], F32)
            pP1 = L2.enter_context(nc.psum_tensor("pP1", [128, 512], F32))
            pPs = [pP0, pP1]
            pR0 = L2.enter_context(nc.psum_tensor("pR0", [128, 512], F32))
            pSQ = L2.enter_context(nc.psum_tensor("pSQ", [128, 1024], F32))
            pX = L2.enter_context(nc.psum_tensor("pX", [128, 512], F32))
            pSt = L2.enter_context(nc.psum_tensor("pSt", [128, 512], F32))

            memset("pool", WA[:], 0.0, ["W"])
            memset("pool", AR[:], 0.0, ["AR_init"])
            load_w(WA[0:64, 0, :], w_up[:, :], 512, 64)
            s = 0
            dma(stg[64:128, s, 0:512], a_up[:, :], ["_"], ["stg%d" % s])
            cp("dve", WA[64:128, 1, :], stg[64:128, s, 0:512], ["stg%d" % s], ["W"])
            rows2 = lambda src: src[0, :].rearrange("(c p) -> c p", p=128)
            dma(rowsT[0:13, :], rows2(mu_shift), ["rowsT"], ["rowsT"])
            dma(rowsT[13:17, :], rows2(w0), [], ["rowsT"])
            dma(rowsT[17:21, :], rows2(a0), [], ["rowsT"])
            dma(rowsT[21:25, :], rows2(k_k), [], ["rowsT"])
            dma(rowsT[25:29, :], rows2(k_a), [], ["rowsT"])
            dma(rowsT[33:37, :], rows2(r_k), [], ["rowsT"])
            dma(rowsT[29:33, :], rows2(k_a), [], ["rowsT"])
            mm(pP0[:, 0:37], rowsT[0:37, :], identf[0:37, 0:37], True, True, ["rowsT", "identf"], ["pP0"])
            cp("dve", cols[:, 0:37], pP0[:, 0:37], ["pP0"], ["cols"])
            ts("dve", cols[:, 29:33], cols[:, 25:29], -1.0, 1.0, ALU.mult, ALU.add, ["cols"], ["cols"])
            dma(finw_bc[:], fin_w[0:1, :].to_broadcast([128, D]), [], ["finw_bc"])
            dma(gnw_bc[:], gn_w[0:1, :].to_broadcast([128, 512]), [], ["gnw_bc"])
            dma(gnb_bc[:], gn_b[0:1, :].to_broadcast([128, 512]), [], ["gnb_bc"])
            memset("pool", mask4[:], 1.0, ["mask4"])
            for q in range(4):
                asel(mask4[:, q, :], [[1, 128]], ALU.is_gt if q % 2 == 0 else ALU.is_ge, -1, 0, "mask4")
            memset("pool", mstT[:], 1.0, ["mstT"])
            asel(mstT[:], [[-1, 128]], ALU.is_gt, 1, 0, "mstT")
            memset("pool", blk[:], 0.0, ["blk"])
            memset("pool", blk[0:64, 0:64], 1.0, ["blk"])
            memset("pool", blk[64:128, 64:128], 1.0, ["blk"])
            memset("pool", E2[:], 0.0, ["E2"])
            memset("pool", E2[0:64, 0:1], 1.0, ["E2"])
            memset("pool", E2[64:128, 1:2], 1.0, ["E2"])
            memset("pool", PF[:], 0.0, ["PF"])
            memset("pool", STf[:], 0.0, ["STf"])
            memset("pool", STb[:], 0.0, ["STb"])

            pSQv = pSQ[:, :].rearrange("p (f l t) -> p f l t", f=2, l=4)
            pTm1 = pSQ[:, 512:1024].bitcast(BF16)
            def pre2(b):
                par = b % 2
                Tprev = blk_T(b - 1) if b > 0 else 0
                T = blk_T(b)
                slot = b % 3
                front(b, slot)
                if b > 0:
                    cp("dve", PF[:, :, 0:1], PF[:, :, Tprev:Tprev + 1], ["PF"], ["PFh"])
                for gi, (c0, n) in enumerate(((0, 4), (4, 4), (8, 4), (12, 1))):
                    pb = pPs[gi % 2]
                    pn = "pP%d" % (gi % 2)
                    pv = proj_fm(pb, pn, c0 * 128, n, T)
                    cp("act", PF[:, c0:c0 + n, 1:1 + T], pv[:, 0:n, 0:T], [pn, "PFh"], ["PF"])
                if b > 0:
                    proj_tm(pP1, "pP1", 1664, 512, T)
                    act(zr[0:T, par, :], pP1[0:T, :], AF.Silu, ["pP1"], ["zr%d" % par])
                yield
                tt("dve", Dl[:, 0:8, 0:T], PF[:, 0:8, 0:T], PF[:, 0:8, 1:1 + T], ALU.subtract, ["PF", "PFh"], ["Dl"])
                tt("pool", Dl[:, 8:13, 0:T], PF[:, 8:13, 0:T], PF[:, 8:13, 1:1 + T], ALU.subtract, ["PF", "PFh"], ["DlB"])
                for c in range(8):
                    stt(Dl[:, c, 0:T], Dl[:, c, 0:T], cols[:, c:c + 1], PF[:, c, 1:1 + T], ALU.mult, ALU.add, ["Dl", "cols", "PF"], ["Dl"])
                tt("pool", Dl[:, 8:13, 0:T], Dl[:, 8:13, 0:T], cols[:, 8:13].unsqueeze(2).to_broadcast([128, 5, T]), ALU.mult,
                   ["DlB", "cols"], ["DlB"])
                tt("pool", Dl[:, 8:13, 0:T], Dl[:, 8:13, 0:T], PF[:, 8:13, 1:1 + T], ALU.add, ["DlB", "PF"], ["DlB"])
                Rv, Kv, Vv = Dl[:, 0:4, 0:T], Dl[:, 4:8, 0:T], Dl[:, 8:12, 0:T]
                yield
                act(LR[0:64, 0:T], Dl[0:64, 12, 0:T], AF.Tanh, ["DlB"], ["LR"])
                act(LR[64:128, 0:T], Dl[64:128, 12, 0:T], AF.Copy, ["DlB"], ["LR2"])
                pv0 = pP0[:, :].rearrange("p (c t) -> p c t", c=4)
                pv1 = pP1[:, :].rearrange("p (c t) -> p c t", c=4)
                for g in range(4):
                    mm(pv0[:, g, 0:T], WA[:, 0, g * 128:(g + 1) * 128], LR[:, 0:T], True, True, ["W", "LR", "LR2"], ["pP0"])
                for g in range(4):
                    mm(pv1[:, g, 0:T], WA[:, 1, g * 128:(g + 1) * 128], LR[:, 0:T], True, True, ["W", "LR", "LR2"], ["pP1"])
                for g in range(4):
                    act(SG[:, g, 0:T], pv0[:, g, 0:T], AF.Sigmoid, ["pP0", "cols"], ["SG"], bias=cols[:, 13 + g:14 + g])
                for g in range(4):
                    act(AVt[:, g, 0:T], pv1[:, g, 0:T], AF.Sigmoid, ["pP1", "cols"], ["AVt"], bias=cols[:, 17 + g:18 + g])
                for g in range(4):
                    P.op("dve", lambda e, g=g, T=T: e.tensor_tensor_scan(out=CUM[:, g, 0:T], data0=ones_f[:, 0:T], data1=SG[:, g, 0:T],
                                                                        initial=0.0, op0=ALU.mult, op1=ALU.add),
                         ["SG", "ones_f"], ["CUM"])
                tt("pool", EXC[:, :, 0:T], CUM[:, :, 0:T], SG[:, :, 0:T], ALU.subtract, ["CUM", "SG"], ["EXC"])
                act(GM[:, :, 0:T], CUM[:, :, 0:T], AF.Exp, ["CUM"], ["GM"], scale=-LAM)
                act(GP[:, :, 0:T], CUM[:, :, 0:T], AF.Exp, ["CUM"], ["GP"], scale=LAM)
                act(GPREV[:, :, 0:T], EXC[:, :, 0:T], AF.Exp, ["EXC"], ["GPREV"], scale=-LAM)
                ts("dve", cols[:, 37:41], CUM[:, :, T - 1], -LAM, None, ALU.mult, None, ["CUM"], ["nb"])
                for g in range(4):
                    act(GHAT[:, g, 0:T], CUM[:, g, 0:T], AF.Exp, ["CUM", "nb"], ["GHAT"], scale=LAM, bias=cols[:, 37 + g:38 + g])
                yield
                bc = lambda c0: cols[:, c0:c0 + 4].unsqueeze(2).to_broadcast([128, 4, T])
                tt("dve", KKS[:, :, 0:T], Kv, bc(21), ALU.mult, ["Dl", "cols"], ["EXC"])
                tt("pool", SQ[:, :, 0:T], KKS[:, :, 0:T], KKS[:, :, 0:T], ALU.mult, ["EXC"], ["SQ"])
                pvR = pP0[:, :].rearrange("p (c t) -> p c t", c=4)
                for g in range(4):
                    mm(pvR[:, g, 0:T], blk[:, :], SQ[:, g, 0:T], True, True, ["SQ", "blk"], ["pP0"])
                act(RN[:, :, 0:T], pvR[:, :, 0:T], AF.Ln, ["pP0"], ["SQ"], bias=1e-12)
                act(RN[:, :, 0:T], RN[:, :, 0:T], AF.Exp, ["SQ"], ["SQ"], scale=-0.5)
                tt("dve", KK[:, :, 0:T], KKS[:, :, 0:T], RN[:, :, 0:T], ALU.mult, ["EXC", "SQ"], ["KK"])
                tt("pool", F1[:, :, 0:T], AVt[:, :, 0:T], bc(25), ALU.mult, ["AVt", "cols"], ["F1"])
                tt("pool", F1[:, :, 0:T], F1[:, :, 0:T], bc(29), ALU.add, ["F1", "cols"], ["F1"])
                tt("dve", KP[:, :, 0:T], Kv, F1[:, :, 0:T], ALU.mult, ["Dl", "F1"], ["KP"])
                tt("pool", BB[:, :, 0:T], KK[:, :, 0:T], AVt[:, :, 0:T], ALU.mult, ["KK", "AVt"], ["SG"])
                yield
                for pr_, (p0, p1) in enumerate(((0, 64), (64, 128))):
                    stt(AR[p0:p1, par, :, pr_, 0, 0:T], KK[p0:p1, :, 0:T], -1.0, GPREV[p0:p1, :, 0:T], ALU.mult, ALU.mult,
                        ["KK", "GPREV", "AR_init"], ["AR%d" % par])
                    tt("dve", AR[p0:p1, par, :, pr_, 1, 0:T], Dl[p0:p1, 0:4, 0:T], GM[p0:p1, :, 0:T], ALU.mult, ["Dl", "GM", "AR_init"], ["AR%d" % par])
                tt("pool", Kt[:, par, :, 0:T], KP[:, :, 0:T], GP[:, :, 0:T], ALU.mult, ["KP", "GP"], ["Kt%d" % par])
                tt("pool", Bt[:, par, :, 0:T], BB[:, :, 0:T], GP[:, :, 0:T], ALU.mult, ["SG", "GP"], ["Bt%d" % par])
                tt("dve", Kh[:, :, 0:T], KP[:, :, 0:T], GHAT[:, :, 0:T], ALU.mult, ["KP", "GHAT"], ["Kh"])
                tt("pool", Bh[:, :, 0:T], BB[:, :, 0:T], GHAT[:, :, 0:T], ALU.mult, ["SG", "GHAT"], ["Bh"])
                cp("act", vb[:, :, 0:T], Vv, ["DlB"], ["vb"])
                cp("dve", gamC[:, par, :], GM[:, :, T - 1], ["GM"], ["gamC%d" % par])
                if b > 0:
                    tt("dve", RK[:, :, 0:T], Rv, KP[:, :, 0:T], ALU.mult, ["Dl", "KP"], ["F1"])
                    tt("pool", RK[:, :, 0:T], RK[:, :, 0:T], bc(33), ALU.mult, ["F1", "cols"], ["F1"])
                    for g in range(4):
                        mm(pP1[0:T, 2 * g:2 * g + 2], RK[:, g, 0:T], E2[:, :], True, True, ["F1", "E2"], ["pP1"])
                    cp("act", sbv[0:T, par, :], pP1[0:T, 0:8], ["pP1"], ["sb%d" % par])
                yield
                pTv4 = pT[:, :].rearrange("p (k c) -> p k c", k=8)
                for k_, (src, sname, dst, dname) in enumerate(((vb, "vb", VT, "VT%d" % par), (Kh, "Kh", KhT, "KhT%d" % par), (Bh, "Bh", BhT, "BhT%d" % par))):
                    for g in range(4):
                        P.op("pe", lambda e, g=g, src=src, T=T: e.transpose(out=pTv4[0:T, g, :], in_=src[:, g, 0:T], identity=ident[:, :]),
                             [sname, "ident"], ["pT"])
                    cp("act" if k_ != 1 else "dve", dst[0:T, par, :], pT[0:T, 0:512], ["pT"], [dname])
                yield

            def main2(b):
                T = blk_T(b)
                par = b % 2
                tg = tail2(b - 1) if b >= 2 else iter(())

                def adv(k):
                    for _ in range(k):
                        if next(tg, "end") == "end":
                            return
                P.set_q(10 ** 9, 1)
                nlev = max(1, int(math.ceil(math.log2(T))))
                ARn = ["AR%d" % par, "AR_init"]
                for h in range(8):
                    g, hp = h // 2, h % 2
                    pb, pbn = (pR0, "pR0") if h % 2 == 0 else (pX, "pX")
                    pr = pb[0:T, :].rearrange("p (q t) -> p q t", q=4)
                    if T == 128:
                        arf = AR[:, par, g, hp, :, :].rearrange("p a t -> p (a t)")
                        mm(pb[0:T, 0:256], Kt[:, par, g, 0:T], arf, True, True, ["Kt%d" % par] + ARn, [pbn])
                        mm(pb[0:T, 256:512], Bt[:, par, g, 0:T], arf, True, True, ["Bt%d" % par] + ARn, [pbn])
                    else:
                        for q in range(2):
                            mm(pr[:, q, 0:T], Kt[:, par, g, 0:T], AR[:, par, g, hp, q, 0:T], True, True, ["Kt%d" % par] + ARn, [pbn])
                            mm(pr[:, 2 + q, 0:T], Bt[:, par, g, 0:T], AR[:, par, g, hp, q, 0:T], True, True, ["Bt%d" % par] + ARn, [pbn])
                    pbt, pbtn = (pSt[0:T, 256:256 + T], "pSt") if h % 2 == 0 else (pSQ[0:T, 0:T], "pSQ0")
                    mm(pbt, AR[:, par, g, hp, 0, 0:T], Bt[:, par, g, 0:T], True, True, ["Bt%d" % par] + ARn, [pbtn])
                    P.give(GV_A)
                    adv(GV_TA)
                    tt("dve", MK[0:T, h, :, 0:T], pr[:, :, 0:T], mask4[0:T, :, 0:T], ALU.mult, [pbn, "mask4"], ["MK%d" % h])
                    tt("dve", MbT[0:T, h, 0:T], pbt, mstT[0:T, 0:T], ALU.mult, [pbtn, "mstT"], ["MbT%d" % h])
                yield
                adv(10 ** 6)
                P.give(GV_X)
                for h in range(8):
                    g = h // 2
                    mm(pX[0:T, h * 64:(h + 1) * 64], AR[:, par, g, h % 2, 0, 0:T], STb[:, g, :], True, False, ARn + ["STb"], ["pX"])
                    mm(pX[0:T, h * 64:(h + 1) * 64], MK[0:T, h, 0, 0:T], VT[0:T, par, h * 64:(h + 1) * 64], False, True,
                       ["MK%d" % h, "VT%d" % par], ["pX"])
                cp("act", Xb[0:T, :, 0, :], pX[0:T, :].rearrange("p (l c) -> p l c", l=8), ["pX"], ["Xb0"])
                for lev in range(nlev):
                    yield
                    pi, po = lev % 2, (lev + 1) % 2
                    for h in range(8):
                        Pk = MK[0:T, h, 2, 0:T] if lev == 0 else PPb[0:T, h, pi, 0, 0:T]
                        pkn = "MK%d" % h if lev == 0 else "PP%d0%d" % (pi, h // 4)
                        mm(pX[0:T, h * 64:(h + 1) * 64], Pk, Xb[0:T, h, pi, :], True, False, [pkn, "Xb%d" % pi], ["pX"])
                        mm(pX[0:T, h * 64:(h + 1) * 64], ident[0:T, 0:T], Xb[0:T, h, pi, :], False, True,
                           ["ident", "Xb%d" % pi], ["pX"])
                    P.give(GV_L)
                    cp("act", Xb[0:T, :, po, :], pX[0:T, :].rearrange("p (l c) -> p l c", l=8), ["pX"], ["Xb%d" % po])
                    if lev < nlev - 1:
                        for a_ in range(2):
                            for half in range(2):
                                for l in range(4):
                                    h = half * 4 + l
                                    if lev == 0:
                                        Pk, PkT = MK[0:T, h, 2, 0:T], MbT[0:T, h, 0:T]
                                        pkn = ["MK%d" % h, "MbT%d" % h]
                                    else:
                                        Pk, PkT = PPb[0:T, h, pi, 0, 0:T], PPb[0:T, h, pi, 1, 0:T]
                                        pkn = ["PP%d0%d" % (pi, half), "PP%d1%d" % (pi, half)]
                                    if a_ == 0:
                                        mm(pSQv[0:T, half, l, 0:T], PkT, Pk, True, True, pkn, ["pSQ%d" % half])
                                    else:
                                        mm(pSQv[0:T, half, l, 0:T], Pk, PkT, True, True, pkn, ["pSQ%d" % half])
                                eng = "act" if (a_ + half) % 2 == 0 else "dve"
                                P.give(GV_S)
                                cp(eng, PPb[0:T, half * 4:half * 4 + 4, po, a_, 0:T], pSQv[0:T, half, :, 0:T],
                                   ["pSQ%d" % half], ["PP%d%d%d" % (po, a_, half)])
                fin = nlev % 2
                xfn = "Xb%d" % fin
                yield
                P.give(GV_Y)
                P.set_q(GV_T, 1)
                for h in range(8):
                    g, base = h // 2, 64 * (h % 2)
                    if b > 0:
                        yo = pR0[0:T, h * 64:(h + 1) * 64]
                        mm(yo, AR[:, par, g, h % 2, 1, 0:T], STb[:, g, :], True, False, ARn + ["STb"], ["pR0"])
                        mm(yo, MK[0:T, h, 1, 0:T], VT[0:T, par, h * 64:(h + 1) * 64], False, False, ["MK%d" % h, "VT%d" % par], ["pR0"])
                        mm(yo, MK[0:T, h, 3, 0:T], Xb[0:T, h, fin, :], False, True, ["MK%d" % h, xfn], ["pR0"])
                    so = pSt[base:base + 64, g * 64:(g + 1) * 64]
                    mm(so, KhT[0:T, par, h * 64:(h + 1) * 64], VT[0:T, par, h * 64:(h + 1) * 64], True, False,
                       ["KhT%d" % par, "VT%d" % par], ["pSt"])
                    mm(so, BhT[0:T, par, h * 64:(h + 1) * 64], Xb[0:T, h, fin, :], False, True, ["BhT%d" % par, xfn], ["pSt"])
                if b > 0:
                    cp("act", Ysb[0:T, :], pR0[0:T, :], ["pR0"], ["YS"])
                tt("dve", STf[:, :, :], STf[:, :, :], gamC[:, par, :].unsqueeze(2).to_broadcast([128, 4, 64]), ALU.mult,
                   ["STf", "gamC%d" % par], ["STf"])
                tt("dve", STf[:, :, :], STf[:, :, :], pSt[:, 0:256].rearrange("p (g c) -> p g c", g=4), ALU.add,
                   ["STf", "pSt"], ["STf"])
                cp("act", STb[:, :, :], STf[:, :, :], ["STf"], ["STb"])
                yield
                if b == NB - 1:
                    for _ in tail2(b):
                        pass

            def tail2(b):
                T = blk_T(b)
                slot = b % 3
                par = b % 2
                xr = "xt%d" % slot
                xo = xsl(slot)
                Y3 = Ysb[0:T, :].rearrange("p (h c) -> p h c", h=8)
                S3 = sqy[0:T, :].rearrange("p (h c) -> p h c", h=8)
                P.op("dve", lambda e, T=T, Y3=Y3: e.tensor_reduce(out=g8[0:T, 0:8], in_=Y3, axis=AX.X, op=ALU.add), ["YS"], ["s1"])
                act(sqy[0:T, :], Ysb[0:T, :], AF.Square, ["YS"], ["YS"])
                yield
                P.op("dve", lambda e, T=T, S3=S3: e.tensor_reduce(out=g8[0:T, 8:16], in_=S3, axis=AX.X, op=ALU.add), ["YS"], ["s2"])
                ts("dve", g8[0:T, 16:24], g8[0:T, 0:8], 1.0 / 64, None, ALU.mult, None, ["s1"], ["mean"])
                yield
                tt("dve", g8[0:T, 24:32], g8[0:T, 16:24], g8[0:T, 16:24], ALU.mult, ["mean"], ["msq"])
                stt(g8[0:T, 32:40], g8[0:T, 8:16], 1.0 / 64, g8[0:T, 24:32], ALU.mult, ALU.subtract, ["s2", "msq"], ["var"])
                yield
                act(g8[0:T, 40:48], g8[0:T, 32:40], AF.Ln, ["var"], ["gsd"], bias=GN_EPS)
                act(g8[0:T, 48:56], g8[0:T, 40:48], AF.Exp, ["gsd"], ["grs"], scale=-0.5)
                yield
                bc8 = lambda c0: g8[0:T, c0:c0 + 8].unsqueeze(2).to_broadcast([T, 8, 64])
                tt("dve", S3, Y3, bc8(16), ALU.subtract, ["YS", "mean"], ["YS"])
                yield
                tt("pool", S3, S3, bc8(48), ALU.mult, ["YS", "grs"], ["YS"])
                yield
                tt("pool", sqy[0:T, :], sqy[0:T, :], gnw_bc[0:T, :], ALU.mult, ["YS", "gnw_bc"], ["YS"])
                yield
                tt("dve", sqy[0:T, :], sqy[0:T, :], gnb_bc[0:T, :], ALU.add, ["YS", "gnb_bc"], ["YS"])
                yield
                tt("pool", Y3, VT[0:T, par, :].rearrange("p (h c) -> p h c", h=8), sbv[0:T, par, :].unsqueeze(2).to_broadcast([T, 8, 64]),
                   ALU.mult, ["VT%d" % par, "sb%d" % par, "YS"], ["YS"])
                yield
                tt("dve", sqy[0:T, :], sqy[0:T, :], Ysb[0:T, :], ALU.add, ["YS"], ["YS"])
                yield
                tt("pool", mixr[0:T, :], sqy[0:T, :], zr[0:T, par, :], ALU.mult, ["YS", "zr%d" % par], ["mixr"])
                yield
                for dc in range(8):
                    src = yfox[0:T, b - 1, dc * 128:(dc + 1) * 128] if dc < 4 else mixr[0:T, (dc - 4) * 128:(dc - 3) * 128]
                    P.op("pe", lambda e, dc=dc, src=src, T=T: e.transpose(out=pTm1[:, dc * 128:dc * 128 + T], in_=src, identity=ident[0:T, 0:T]),
                         ["mixr", "ident"], ["pSQ1"])
                    if dc % 4 == 3:
                        yield
                cp("act", mixT[:, :, 0:T], pTm1[:, :].rearrange("p (c t) -> p c t", c=8)[:, :, 0:T], ["pSQ1"], ["mixT"])
                yield
                for hf in range(2):
                    pb, pn = pSQ[:, 512:1024], "pSQ1"
                    for dc in range(8):
                        mm(pb[0:T, :], mixT[:, dc, 0:T], wout[:, dc, hf * 512:(hf + 1) * 512], dc == 0, dc == 7, ["mixT", "W"], [pn])
                    yield
                    tt("dve", Hh[0:T, hf * 512:(hf + 1) * 512], pb[0:T, :], xo[0:T, hf * 512:(hf + 1) * 512], ALU.add,
                       [pn, xr, "mixr"], ["YS"])
                    yield
                act(xs[0:T, :], Hh[0:T, :], AF.Square, ["YS"], ["xs", "ss2"], accum=st_[0:T, 4:5])
                yield
                act(st_[0:T, 5:6], st_[0:T, 4:5], AF.Ln, ["ss2"], ["sd2"], bias=NORM_EPS, scale=1.0 / D)
                act(st_[0:T, 6:7], st_[0:T, 5:6], AF.Exp, ["sd2"], ["rstd2"], scale=-0.5)
                yield
                stt(xo[0:T, :], Hh[0:T, :], st_[0:T, 6:7], finw_bc[0:T, :], ALU.mult, ALU.mult,
                    ["YS", "rstd2", "finw_bc", xr], [xr])
                dma(out[(b - 1) * 128:b * 128, :], xo[0:T, :], [xr], ["out%d" % b] + (["stg0"] if slot == 2 else []), final=True)
                yield

            if stage > 3:
                for _ in pre2(0):
                    pass
                for b in range(NB):
                    P.run_pair(drain(main2(b)), drain(pre2(b + 1) if b + 1 < NB else iter(())), Q2A, Q2B)
    P.emit()
    G.close()
    return nc


_NAMES = ["meta", "norm_w", "w_in", "b_f", "mu_shift", "w0", "w_up", "a0", "a_up", "k_k", "k_a", "r_k",
          "gn_w", "gn_b", "w_out", "final_norm_w"]


def _prep(inputs, nblk_real):
    f = lambda a: np.ascontiguousarray(np.asarray(a, dtype=np.float32))
    shared = {
        "meta": f(inputs["meta"]),
        "norm_w": f(inputs["norm_w"]).reshape(1, D),
        "w_in": f(inputs["w_in"]).reshape(D, 4232),
        "b_f": f(inputs["b_f"]).reshape(1, 8),
        "mu_shift": f(inputs["mu_shift"]).reshape(1, 1664),
        "w0": f(inputs["w0"]).reshape(1, 512),
        "w_up": f(inputs["w_up"]).reshape(64, 512),
        "a0": f(inputs["a0"]).reshape(1, 512),
        "a_up": f(inputs["a_up"]).reshape(64, 512),
        "k_k": f(inputs["k_k"]).reshape(1, 512),
        "k_a": f(inputs["k_a"]).reshape(1, 512),
        "r_k": f(inputs["r_k"]).reshape(1, 512),
        "gn_w": f(inputs["gn_w"]).reshape(1, 512),
        "gn_b": f(inputs["gn_b"]).reshape(1, 512),
        "w_out": f(inputs["w_out"]).reshape(D, D),
        "final_norm_w": f(inputs["final_norm_w"]).reshape(1, D),
    }
    xs = f(inputs["x"])
    maps = []
    for c in range(xs.shape[0]):
        m = dict(shared)
        m["x"] = np.ascontiguousarray(xs[c, :128 * nblk_real])
        maps.append(m)
    return maps


def kernel(**inputs):
    nblk_real = inputs["x"].shape[1] // 128
    nc = build(nblk_real)
    maps = _prep(inputs, nblk_real)
    ncore = len(maps)
    res = run_bass_kernel_spmd(nc, maps, core_ids=list(range(ncore)))
    return np.stack([np.asarray(r["out"], dtype=np.float32) for r in res.results], axis=0)
```

```python
import contextlib
import math
import threading
import numpy as np
import concourse.bass as bass
import concourse.mybir as mybir
from concourse.bass_utils import run_bass_kernel_spmd

F32 = mybir.dt.float32
BF16 = mybir.dt.bfloat16
AF = mybir.ActivationFunctionType
ALU = mybir.AluOpType
AX = mybir.AxisListType

ENGS = ("pe", "act", "dve", "pool", "sp")
NDMASEM = 24
LAM = math.exp(-0.5)
NORM_EPS = 1e-6
GN_EPS = 64e-5


class _Op:
    __slots__ = ("eng", "fn", "deps", "needs_inc", "val", "is_dma", "dma_slot", "dma_val", "sid")

    def __init__(self, eng, fn):
        self.eng, self.fn = eng, fn
        self.deps = []
        self.needs_inc = False
        self.val = None
        self.is_dma = False
        self.dma_slot = None
        self.dma_val = None
        self.sid = None


class Prog:
    def __init__(self, nc):
        self.nc = nc
        self.ops = {e: [] for e in ENGS}
        self.last_w = {}
        self.readers = {}
        self.ndma = 0
        self.dma_last = [None] * NDMASEM
        self.final_waits = []
        self.hook = None
        self.pe_ok = set()
        self._tl = threading.local()
        self._npair = 0

    def run_pair(self, fa, fb, qa=1, qb=1):
        cv = threading.Condition()
        st = {"turn": 0, "done": [False, False], "cnt": 0, "err": None}
        quota = [qa, qb]
        self.quota = quota
        tl = threading.local()
        self._pair = (cv, st, quota, tl)

        def hook():
            me = tl.idx
            with cv:
                st["cnt"] += 1
                if st["cnt"] >= quota[me] and not st["done"][1 - me]:
                    st["cnt"] = 0
                    st["turn"] = 1 - me
                    cv.notify_all()
                    while st["turn"] != me:
                        cv.wait()

        self._npair += 1
        npair = self._npair

        def runner(idx, f):
            tl.idx = idx
            self._tl.sid = (npair, idx)
            with cv:
                while st["turn"] != idx:
                    cv.wait()
            try:
                f()
            except BaseException as ex:
                st["err"] = ex
            finally:
                with cv:
                    st["done"][idx] = True
                    st["cnt"] = 0
                    st["turn"] = 1 - idx
                    cv.notify_all()

        self.hook = hook
        ta = threading.Thread(target=runner, args=(0, fa))
        tb = threading.Thread(target=runner, args=(1, fb))
        ta.start(); tb.start(); ta.join(); tb.join()
        self.hook = None
        self._pair = None
        if st["err"] is not None:
            raise st["err"]

    def give(self, k):
        if getattr(self, "_pair", None) is None or k <= 0:
            return
        cv, st, quota, tl = self._pair
        me = tl.idx
        with cv:
            if st["done"][1 - me]:
                return
            quota[1 - me] = k
            st["cnt"] = 0
            st["turn"] = 1 - me
            cv.notify_all()
            while st["turn"] != me:
                cv.wait()

    def set_q(self, qa, qb):
        if getattr(self, "quota", None) is not None:
            self.quota[0], self.quota[1] = qa, qb

    def _dep(self, op, d):
        if d is None or d is op:
            return
        if op.eng == "pe" and d.eng == "pe" and not d.is_dma:
            return
        if d not in op.deps:
            op.deps.append(d)
        d.needs_inc = True

    def op(self, eng, fn, reads=(), writes=(), pe_start=False):
        o = _Op(eng, fn)
        o.sid = getattr(self._tl, "sid", None)
        self.ops[eng].append(o)
        if eng == "pe" and pe_start:
            for w in writes:
                lw = self.last_w.get(w)
                if (lw is not None and lw.eng == "pe" and not lw.is_dma and not self.readers.get(w)
                        and lw.sid != getattr(self._tl, "sid", None)):
                    raise RuntimeError("PSUM resource %r: new matmul group over unread PE data" % (w,))
        for r in reads:
            self._dep(o, self.last_w.get(r))
        for w in writes:
            self._dep(o, self.last_w.get(w))
            for rd in self.readers.get(w, ()):
                self._dep(o, rd)
        for r in reads:
            self.readers.setdefault(r, []).append(o)
        for w in writes:
            self.last_w[w] = o
            self.readers[w] = []
        if self.hook is not None:
            self.hook()
        return o

    def dma(self, eng, fn, reads=(), writes=(), final=False):
        o = self.op(eng, fn, reads, writes)
        o.is_dma = True
        slot = self.ndma % NDMASEM
        o.dma_slot = slot
        o.dma_val = 16 * (self.ndma // NDMASEM + 1)
        prev = self.dma_last[slot]
        if prev is not None and prev not in o.deps:
            o.deps.append(prev)
        self.dma_last[slot] = o
        self.ndma += 1
        if final:
            self.final_waits.append(o)
        return o

    def barrier(self):
        lasts = [self.ops[e][-1] for e in ENGS if self.ops[e]]
        lasts += [d for d in self.dma_last if d is not None]
        for e in ENGS:
            o = _Op(e, None)
            self.ops[e].append(o)
            for d in lasts:
                if d is o:
                    continue
                if d not in o.deps:
                    o.deps.append(d)
                if not d.is_dma:
                    d.needs_inc = True
        self.last_w = {}
        self.readers = {}

    def emit(self):
        nc = self.nc
        with contextlib.ExitStack() as st:
            sems = {e: st.enter_context(nc.semaphore("s_" + e)) for e in ENGS}
            dsems = [st.enter_context(nc.semaphore("d_%d" % i)) for i in range(NDMASEM)]
            for e in ENGS:
                c = 0
                for o in self.ops[e]:
                    if o.is_dma or o.fn is None:
                        continue
                    if o.needs_inc:
                        c += 1
                        o.val = c
            block = st.enter_context(nc.Block())
            engobj = {"pe": "tensor", "act": "scalar", "dve": "vector", "pool": "gpsimd", "sp": "sync"}

            def make(e):
                def body(engine):
                    waited = {}
                    for o in self.ops[e]:
                        for d in o.deps:
                            if d.is_dma:
                                key, sem, v = ("d", d.dma_slot), dsems[d.dma_slot], d.dma_val
                            else:
                                key, sem, v = d.eng, sems[d.eng], d.val
                            if waited.get(key, 0) >= v:
                                continue
                            engine.wait_ge(sem, v)
                            waited[key] = v
                        if o.fn is None:
                            continue
                        ins = o.fn(engine)
                        if o.is_dma:
                            ins.then_inc(dsems[o.dma_slot], 16)
                        elif o.needs_inc:
                            ins.then_inc(sems[e], 1)
                    if e == "sp":
                        for d in self.final_waits:
                            engine.wait_ge(dsems[d.dma_slot], d.dma_val)
                return body

            for e in ENGS:
                getattr(block, engobj[e])(make(e))


D = 1024
NHEAD = 8
HD = 64
C_FOX = 2056
C_RWKV = 2176


class _Stop(Exception):
    pass


Q1A, Q1B, Q2A, Q2B = 4, 1, 2, 1
BSW = 21
GV_A, GV_X, GV_L, GV_S, GV_Y, GV_T, GV_TA = 16, 6, 3, 2, 6, 2, 1


def build(nblk_real=32, debug=False, stage=99):
    NB = nblk_real + 1
    LTOT = 16 + 128 * nblk_real
    nc = bass.Bass("TRN2", target_bir_lowering=False)
    dt_in = lambda n, s: nc.dram_tensor(n, s, F32, kind="ExternalInput").ap()
    x = dt_in("x", [128 * nblk_real, D])
    meta = dt_in("meta", [16, D])
    norm_w = dt_in("norm_w", [1, D])
    w_in = dt_in("w_in", [D, 4232])
    b_f = dt_in("b_f", [1, 8])
    mu_shift = dt_in("mu_shift", [1, 1664])
    w0 = dt_in("w0", [1, 512])
    w_up = dt_in("w_up", [64, 512])
    a0 = dt_in("a0", [1, 512])
    a_up = dt_in("a_up", [64, 512])
    k_k = dt_in("k_k", [1, 512])
    k_a = dt_in("k_a", [1, 512])
    r_k = dt_in("r_k", [1, 512])
    gn_w = dt_in("gn_w", [1, 512])
    gn_b = dt_in("gn_b", [1, 512])
    w_out = dt_in("w_out", [D, D])
    fin_w = dt_in("final_norm_w", [1, D])
    out = nc.dram_tensor("out", [128 * nblk_real, D], F32, kind="ExternalOutput").ap()
    dbg = None
    if debug:
        dbg = nc.dram_tensor("dbg", [128 * nblk_real, D], F32, kind="ExternalOutput").ap()

    P = Prog(nc)

    def mm(o, lhsT, rhs, start, stop, rd, wr):
        P.op("pe", lambda e: e.matmul(o, lhsT=lhsT, rhs=rhs, start=start, stop=stop, skip_group_check=True), rd, wr, pe_start=start)

    def act(o, i, func, rd, wr, bias=None, scale=None, accum=None):
        kw = {}
        if bias is not None:
            kw["bias"] = bias
        if scale is not None:
            kw["scale"] = scale
        if accum is not None:
            kw["accum_out"] = accum
        P.op("act", lambda e: e.activation(out=o, in_=i, func=func, **kw), rd, wr)

    def tt(eng, o, a, b, op, rd, wr):
        P.op(eng, lambda e: e.tensor_tensor(out=o, in0=a, in1=b, op=op), rd, wr)

    def ts(eng, o, a, s1, s2, op0, op1, rd, wr):
        if s2 is None:
            P.op(eng, lambda e: e.tensor_scalar(out=o, in0=a, scalar1=s1, scalar2=None, op0=op0), rd, wr)
        else:
            P.op(eng, lambda e: e.tensor_scalar(out=o, in0=a, scalar1=s1, scalar2=s2, op0=op0, op1=op1), rd, wr)

    def stt(o, a, s, b, op0, op1, rd, wr):
        P.op("dve", lambda e: e.scalar_tensor_tensor(out=o, in0=a, scalar=s, in1=b, op0=op0, op1=op1), rd, wr)

    def cp(eng, o, i, rd, wr):
        if eng == "act":
            P.op("act", lambda e: e.activation(out=o, in_=i, func=AF.Copy), rd, wr)
        else:
            P.op(eng, lambda e: e.tensor_copy(out=o, in_=i), rd, wr)

    def memset(eng, o, v, wr):
        P.op(eng, lambda e: e.memset(o, v), (), wr)

    def asel(o, pattern, cmp, cm, base, name):
        P.op("pool", lambda e: e.affine_select(out=o, in_=o, pattern=pattern, compare_op=cmp, fill=0.0,
                                               base=base, channel_multiplier=cm), [name], [name])

    def dma(o, i, rd, wr, final=False, slow=False):
        if slow:
            P.dma("sp", lambda e: e.dma_start(out=o, in_=i, allow_slow_non_contiguous=True), rd, wr, final=final)
        else:
            P.dma("sp", lambda e: e.dma_start(out=o, in_=i), rd, wr, final=final)

    def blk_T(b):
        return 16 if b == 0 else 128

    def blk_t0(b):
        return 0 if b == 0 else 16 + 128 * (b - 1)

    G = contextlib.ExitStack()
    if True:
        SB = lambda n, s, d: G.enter_context(nc.sbuf_tensor(n, s, d))
        PS = lambda n, s, d: G.enter_context(nc.psum_tensor(n, s, d))
        W = SB("W", [128, 8, C_RWKV], BF16)
        stg = SB("stg", [128, 1, 1088], F32)
        yfox = SB("yfox", [128, nblk_real, 512], BF16)
        ident = SB("ident", [128, 128], BF16)
        identf = SB("identf", [128, 128], F32)
        mincl_f = SB("mincl_f", [128, 128], F32)
        ones_f = SB("ones_f", [128, 128], F32)
        normw_col = SB("normw_col", [128, 8], F32)
        xt = SB("xt", [128, 2, D], F32)
        xs = SB("xs", [128, D], BF16)
        uT = SB("uT", [128, 8, 128], BF16)
        st_ = SB("stats", [128, 16], F32)
        pP0 = PS("pP0", [128, 512], F32)
        pT = PS("pT", [128, 1024], BF16)

        memset("pool", identf[:], 1.0, ["identf"])
        asel(identf[:], [[-1, 128]], ALU.is_equal, 1, 0, "identf")
        cp("dve", ident[:], identf[:], ["identf"], ["ident"])
        memset("pool", mincl_f[:], 1.0, ["mincl_f"])
        asel(mincl_f[:], [[1, 128]], ALU.is_ge, -1, 0, "mincl_f")
        memset("pool", ones_f[:], 1.0, ["ones_f"])
        rowsT = SB("rowsT", [48, 128], F32)
        dma(rowsT[0:8, :], norm_w[0, :].rearrange("(c p) -> c p", p=128), [], ["rowsT"])
        mm(pP0[:, 0:8], rowsT[0:8, :], identf[0:8, 0:8], True, True, ["rowsT", "identf"], ["pP0"])
        cp("dve", normw_col[:, 0:8], pP0[:, 0:8], ["pP0"], ["normw_col"])

        wq = [0]
        stg_slots = [(stg[:, 0, :], "stg0")]
        wout = SB("wout", [128, 8, D], BF16)

        def load_w(dst_fn, src_fn, ncols, nparts=128):
            i = wq[0]
            wq[0] += 1
            sap, sname = stg_slots[i % len(stg_slots)]
            dma(sap[0:nparts, 0:ncols], src_fn, ["_"], [sname])
            eng = ("act", "dve", "pool")[i % 3]
            cp(eng, dst_fn, sap[0:nparts, 0:ncols], [sname], ["W"])

        def xsl(slot):
            return xt[:, slot, :] if slot < 2 else stg[:, 0, 0:D]

        def front(b, slot):
            T = blk_T(b)
            src = meta[:, :] if b == 0 else x[(b - 1) * 128:b * 128, :]
            xr = "xt%d" % slot
            dma(xsl(slot)[0:T, :], src, ["stg0"] if slot == 2 else [], [xr, "stg0"] if slot == 2 else [xr])
            act(xs[0:T, :], xsl(slot)[0:T, :], AF.Square, [xr], ["xs", "ss"], accum=st_[0:T, 0:1])
            act(st_[0:T, 1:2], st_[0:T, 0:1], AF.Ln, ["ss"], ["sd"], bias=NORM_EPS, scale=1.0 / D)
            act(st_[0:T, 2:3], st_[0:T, 1:2], AF.Exp, ["sd"], ["rstd"], scale=-0.5)
            act(xs[0:T, :], xsl(slot)[0:T, :], AF.Copy, [xr, "rstd"], ["xs"], scale=st_[0:T, 2:3])
            for dc in range(8):
                P.op("pe", lambda e, dc=dc: e.transpose(out=pT[:, dc * 128:dc * 128 + T],
                                                        in_=xs[0:T, dc * 128:(dc + 1) * 128],
                                                        identity=ident[0:T, 0:T]), ["xs", "ident"], ["pT"])
            pTv = pT[:, :].rearrange("p (c t) -> p c t", c=8)
            tt("dve", uT[:, :, 0:T], pTv[:, :, 0:T], normw_col[:, 0:8].unsqueeze(2).to_broadcast([128, 8, T]),
               ALU.mult, ["pT", "normw_col"], ["uT"])

        def proj_fm(pb, name, col0, nch, T):
            pv = pb[:, :].rearrange("p (c t) -> p c t", c=4)
            for c in range(nch):
                for dc in range(8):
                    mm(pv[:, c, 0:T], W[:, dc, col0 + c * 128:col0 + (c + 1) * 128], uT[:, dc, 0:T],
                       dc == 0, dc == 7, ["W", "uT"], [name])
            return pv

        def proj_tm(pb, name, col0, ncol, T):
            for dc in range(8):
                mm(pb[0:T, 0:ncol], uT[:, dc, 0:T], W[:, dc, col0:col0 + ncol], dc == 0, dc == 7, ["W", "uT"], [name])

        with contextlib.ExitStack() as L1:
            S1 = lambda n, s, d: L1.enter_context(nc.sbuf_tensor(n, s, d))
            KT = S1("KT", [128, 4, LTOT], BF16)
            Vsb = S1("Vsb", [128, NB, 8, 65], BF16)
            extK = S1("extK", [128, LTOT], BF16)
            QT = S1("QT", [128, 2, 4, 2, 128], BF16)
            Rext = S1("Rext", [128, 2, 8, 128], BF16)
            hmask = S1("hmask", [48, 8, 128], BF16)
            PT = S1("PT", [128, 2, 8, 128], BF16)
            mincl_b = S1("mincl_b", [128, 128], BF16)
            ZY = S1("ZY", [128, 2, 512], F32)
            zf = ZY
            yft = S1("yft", [128, 512], F32)
            hmaskf = ZY[0:48, :, :].rearrange("p a (h t) -> p (a h) t", h=4)
            PK = S1("PK", [128, 8, 6], BF16)
            PQ = S1("PQ", [128, 8, 6], BF16)
            sm = S1("sm", [128, 64], F32)
            Aacc = S1("Aacc", [128, 8], F32)
            bf_bc = S1("bf_bc", [128, 8], F32)
            zeros_b = S1("zeros_b", [128, 512], BF16)
            stg1 = S1("stg1", [128, 2, 1088], F32)
            for k_ in range(2):
                stg_slots.append((stg1[:, k_, :], "stg%d" % (k_ + 1)))
            pS = L1.enter_context(nc.psum_tensor("pS", [128, 2048], F32))
            pO = L1.enter_context(nc.psum_tensor("pO", [128, 1024], F32))

            for dc in range(8):
                for hf in range(2):
                    load_w(W[:, dc, hf * 1028:(hf + 1) * 1028], w_in[dc * 128:(dc + 1) * 128, hf * 1028:(hf + 1) * 1028], 1028)
            cp("dve", mincl_b[:], mincl_f[:], ["mincl_f"], ["mincl_b"])
            memset("pool", hmaskf, 1.0, ["ZY"])
            asel(hmaskf, [[-6, 8], [0, 128]], ALU.is_ge, 1, 0, "ZY")
            asel(hmaskf, [[6, 8], [0, 128]], ALU.is_ge, -1, 5, "ZY")
            cp("dve", hmask[:], hmaskf, ["ZY"], ["hmask"])
            memset("pool", Vsb[:, :, :, 64:65], 1.0, ["Vones"])
            memset("pool", PK[:], 1.0, ["PK"])
            memset("pool", PQ[:], 1.0, ["PQ"])
            memset("pool", Aacc[:], 0.0, ["Aacc"])
            memset("pool", zeros_b[:], 0.0, ["zeros_b"])
            memset("pool", extK[:], 0.0, ["eK_init"])
            memset("pool", Rext[:], 0.0, ["Rext_init"])
            memset("pool", QT[:], 0.0, ["QT_init"])
            dma(bf_bc[:], b_f[0:1, :].to_broadcast([128, 8]), [], ["bf_bc"])

            pSv2 = pS[:, :].rearrange("p (s h q) -> p s h q", s=2, h=8)
            pOv = pO[:, :].rearrange("p (h c) -> p h c", h=8)
            def pre1(b):
                T = blk_T(b)
                t0 = blk_t0(b)
                par = b % 2
                two = (b - 1) <= BSW
                pPx, pPxn = (pS[:, 1536:2048], "pS11") if two else (pP0, "pP0")
                front(b, b % 2)
                yield
                if b > 0:
                    pv = proj_fm(pP0, "pP0", 0, 4, T)
                    act(QT[0:64, par, :, 0, 0:T], pv[0:64, :, 0:T], AF.Copy, ["pP0", "QT_init"], ["QT%d" % par], scale=0.125)
                    act(QT[64:128, par, :, 1, 0:T], pv[64:128, :, 0:T], AF.Copy, ["pP0", "QT_init"], ["QT%d" % par], scale=0.125)
                    yield
                pv = proj_fm(pPx, pPxn, 512, 4, T)
                cp("dve", KT[:, :, t0:t0 + T], pv[:, :, 0:T], [pPxn], ["KT%d" % b])
                yield
                proj_tm(pP0, "pP0", 1024, 512, T)
                cp("act", Vsb[0:T, b, :, 0:64], pP0[0:T, :].rearrange("p (h c) -> p h c", h=8), ["pP0"], ["V%d" % b])
                proj_tm(pPx, pPxn, 1536, 8, T)
                tt("dve", sm[0:T, 0:8], pPx[0:T, 0:8], bf_bc[0:T, :], ALU.add, [pPxn, "bf_bc"], ["xf"])
                yield
                if b > 0:
                    proj_tm(pP0, "pP0", 1544, 512, T)
                    act(zf[0:T, par, :], pP0[0:T, :], AF.Silu, ["pP0"], ["zf%d" % par, "ZY"])
                    yield
                act(sm[0:T, 8:16], sm[0:T, 0:8], AF.Exp, ["xf"], ["e1"], scale=-1.0)
                act(sm[0:T, 16:24], sm[0:T, 8:16], AF.Ln, ["e1"], ["lfn"], bias=1.0)
                mm(pPx[0:T, 16:24], mincl_f[0:T, 0:T], sm[0:T, 16:24], True, False, ["lfn", "mincl_f"], [pPxn])
                mm(pPx[0:T, 16:24], ones_f[:, 0:T], Aacc[:, :], False, True, ["Aacc", "ones_f"], [pPxn])
                cp("act", sm[0:T, 24:32], pPx[0:T, 16:24], [pPxn], ["cpos"])
                tt("dve", Aacc[0:T, :], Aacc[0:T, :], sm[0:T, 16:24], ALU.add, ["Aacc", "lfn"], ["Aacc"])
                cp("dve", PK[0:T, :, 0], sm[0:T, 24:32], ["cpos"], ["PK"])
                tt("dve", sm[0:T, 32:40], sm[0:T, 24:32], PK[0:T, :, 0], ALU.subtract, ["cpos", "PK"], ["r1"])
                cp("dve", PK[0:T, :, 1], sm[0:T, 32:40], ["r1"], ["PK"])
                tt("dve", sm[0:T, 40:48], sm[0:T, 32:40], PK[0:T, :, 1], ALU.subtract, ["r1", "PK"], ["r2"])
                cp("dve", PK[0:T, :, 2], sm[0:T, 40:48], ["r2"], ["PK"])
                yield
                P.op("pe", lambda e, T=T: e.transpose(out=pT[0:48, 0:T], in_=PK[0:T, :, :].rearrange("p h i -> p (h i)"),
                                                      identity=ident[0:T, 0:T]), ["PK", "ident"], ["pT"])
                cp("act", extK[0:48, t0:t0 + T], pT[0:48, 0:T], ["pT", "eK_init"], ["eK%d" % b])
                if b == 0:
                    return
                ts("dve", PQ[0:T, :, 3:6], PK[0:T, :, 0:3], -1.0, None, ALU.mult, None, ["PK"], ["PQ"])
                P.op("pe", lambda e, T=T: e.transpose(out=pT[0:48, 128:128 + T], in_=PQ[0:T, :, :].rearrange("p h i -> p (h i)"),
                                                      identity=ident[0:T, 0:T]), ["PQ", "ident"], ["pT"])
                tt("dve", Rext[0:48, par, :, 0:T], pT[0:48, 128:128 + T].unsqueeze(1).to_broadcast([48, 8, T]), hmask[:, :, 0:T],
                   ALU.mult, ["pT", "hmask", "Rext_init"], ["Rext%d" % par])
                yield

            def main1(b):
                if b == 0:
                    return
                T = blk_T(b)
                par = b % 2
                for hf in range(2):
                    mm(pO[0:T, hf * 512:(hf + 1) * 512], zeros_b[:, 0:T], zeros_b[:, 0:512], True, False,
                       ["zeros_b"], ["pO"])
                def s_part(kb):
                    Tk = blk_T(kb)
                    k0 = blk_t0(kb)
                    sl = kb % 2
                    ptn = "PT%d" % sl
                    ss_ = sl if b > BSW else 0
                    pSv = pSv2[:, ss_]
                    for hf in range(2):
                        psn = "pS%d%d" % (ss_, hf)
                        mm(pSv[0:Tk, hf * 4:(hf + 1) * 4, 0:T], extK[:, k0:k0 + Tk], Rext[:, par, hf * 4:(hf + 1) * 4, 0:T],
                           True, False, ["eK%d" % kb, "Rext%d" % par, "Rext_init", "eK_init"], [psn])
                        for hh in range(4):
                            h = hf * 4 + hh
                            g = h // 2
                            mm(pSv[0:Tk, h, 0:T], KT[:, g, k0:k0 + Tk], QT[:, par, g, h % 2, 0:T],
                               False, hh == 3, ["KT%d" % kb, "QT%d" % par, "QT_init"], [psn])
                        act(PT[0:Tk, sl, hf * 4:(hf + 1) * 4, 0:T], pSv[0:Tk, hf * 4:(hf + 1) * 4, 0:T], AF.Exp, [psn], [ptn])
                    if kb == b:
                        tt("dve", PT[0:Tk, sl, :, 0:T], PT[0:Tk, sl, :, 0:T],
                           mincl_b[0:Tk, 0:T].unsqueeze(1).to_broadcast([Tk, 8, T]), ALU.mult, [ptn, "mincl_b"], [ptn])

                def pv_part(kb):
                    Tk = blk_T(kb)
                    sl = kb % 2
                    ptn = "PT%d" % sl
                    for h in range(8):
                        mm(pOv[0:T, h, 0:65], PT[0:Tk, sl, h, 0:T], Vsb[0:Tk, kb, h, 0:65], False, kb == b,
                           [ptn, "V%d" % kb, "Vones"], ["pO"])

                s_part(0)
                for kb in range(b + 1):
                    if kb + 1 <= b:
                        s_part(kb + 1)
                    pv_part(kb)
                    yield
                P.op("dve", lambda e, T=T: e.reciprocal(out=sm[0:T, 48:56], in_=pOv[0:T, :, 64]), ["pO"], ["rden"])
                tt("dve", yft[0:T, :].rearrange("p (h c) -> p h c", h=8), pOv[0:T, :, 0:64],
                   sm[0:T, 48:56].unsqueeze(2).to_broadcast([T, 8, 64]), ALU.mult, ["pO", "rden"], ["yft"])
                tt("dve", yfox[0:T, b - 1, :], yft[0:T, :], zf[0:T, par, :], ALU.mult, ["yft", "zf%d" % par], ["yfox"])
                yield

            def interleave(gm, gp):
                dm = dp = False
                while not (dm and dp):
                    if not dm:
                        try:
                            next(gm)
                        except StopIteration:
                            dm = True
                    if not dp:
                        try:
                            next(gp)
                        except StopIteration:
                            dp = True

            if stage > 1:
                for _ in pre1(0):
                    pass
                drain = lambda g: (lambda: [None for _ in g])
                def load_w2():
                    for dc in range(8):
                        for hf in range(2):
                            load_w(W[:, dc, hf * 1088:(hf + 1) * 1088],
                                   w_in[dc * 128:(dc + 1) * 128, C_FOX + hf * 1088:C_FOX + (hf + 1) * 1088], 1088)
                    for dc in range(8):
                        load_w(wout[:, dc, :], w_out[dc * 128:(dc + 1) * 128, :], 1024)

                for b in range(NB):
                    if b + 1 < NB:
                        P.run_pair(drain(main1(b)), drain(pre1(b + 1)), Q1A, Q1B)
                    else:
                        P.run_pair(drain(main1(b)), load_w2, 12, 1)

        del stg_slots[1:]
        P.barrier()

        with contextlib.ExitStack() as L2:
          if stage >= 3:
            S2 = lambda n, s, d: L2.enter_context(nc.sbuf_tensor(n, s, d))
            WA = S2("WA", [128, 2, 512], BF16)
            PF = S2("PF", [128, 13, 129], F32)
            Dl = S2("Dl", [128, 13, 128], F32)
            f32t = lambda n: S2(n, [128, 4, 128], F32)
            SG, AVt, CUM, EXC = f32t("SG"), f32t("AVt"), f32t("CUM"), f32t("EXC")
            GM, GP, GPREV, GHAT = f32t("GM"), f32t("GP"), f32t("GPREV"), f32t("GHAT")
            SQ, KK, F1, KP = f32t("SQ"), f32t("KK"), f32t("F1"), f32t("KP")
            KKS, RN, RK, BB = EXC, SQ, F1, SG
            LR = S2("LR", [128, 128], BF16)
            AR = S2("AR", [128, 2, 4, 2, 2, 128], BF16)
            Kt = S2("Kt", [128, 2, 4, 128], BF16)
            Bt = S2("Bt", [128, 2, 4, 128], BF16)
            Kh = S2("Kh", [128, 4, 128], BF16)
            Bh = S2("Bh", [128, 4, 128], BF16)
            vb = S2("vb", [128, 4, 128], BF16)
            VT = S2("VT", [128, 2, 512], BF16)
            KhT = S2("KhT", [128, 2, 512], BF16)
            BhT = S2("BhT", [128, 2, 512], BF16)
            MK = S2("MK", [128, 8, 4, 128], BF16)
            MbT = S2("MbT", [128, 8, 128], BF16)
            PPb = S2("PPb", [128, 8, 2, 2, 128], BF16)
            Xb = S2("Xb", [128, 8, 2, 64], BF16)
            STf = S2("STf", [128, 4, 64], F32)
            STb = S2("STb", [128, 4, 64], BF16)
            YS = S2("YS", [128, 1024], F32)
            Ysb = YS[:, 0:512]
            sqy = YS[:, 512:1024]
            Hh = YS
            zr = S2("zr", [128, 2, 512], F32)
            sbv = S2("sbv", [128, 2, 8], F32)
            gamC = S2("gamC", [128, 2, 4], F32)
            mixr = S2("mixr", [128, 512], BF16)
            mixT = S2("mixT", [128, 8, 128], BF16)
            mask4 = S2("mask4", [128, 4, 128], F32)
            mstT = S2("mstT", [128, 128], F32)
            blk = S2("blk", [128, 128], F32)
            E2 = S2("E2", [128, 2], F32)
            finw_bc = S2("finw_bc", [128, D], F32)
            gnw_bc = S2("gnw_bc", [128, 512], F32)
            gnb_bc = S2("gnb_bc", [128, 512], F32)
            cols = S2("cols", [128, 48], F32)
            g8 = S2("g8", [128, 64], F32)
            pP1 = L2.enter_context(nc.psum_tensor("pP1", [128, 512], F32))
            pPs = [pP0, pP1]
            pR0 = L2.enter_context(nc.psum_tensor("pR0", [128, 512], F32))
            pSQ = L2.enter_context(nc.psum_tensor("pSQ", [128, 1024], F32))
            pX = L2.enter_context(nc.psum_tensor("pX", [128, 512], F32))
            pSt = L2.enter_context(nc.psum_tensor("pSt", [128, 512], F32))

            memset("pool", WA[:], 0.0, ["W"])
            memset("pool", AR[:], 0.0, ["AR_init"])
            load_w(WA[0:64, 0, :], w_up[:, :], 512, 64)
            s = 0
            dma(stg[64:128, s, 0:512], a_up[:, :], ["_"], ["stg%d" % s])
            cp("dve", WA[64:128, 1, :], stg[64:128, s, 0:512], ["stg%d" % s], ["W"])
            rows2 = lambda src: src[0, :].rearrange("(c p) -> c p", p=128)
            dma(rowsT[0:13, :], rows2(mu_shift), ["rowsT"], ["rowsT"])
            dma(rowsT[13:17, :], rows2(w0), [], ["rowsT"])
            dma(rowsT[17:21, :], rows2(a0), [], ["rowsT"])
            dma(rowsT[21:25, :], rows2(k_k), [], ["rowsT"])
            dma(rowsT[25:29, :], rows2(k_a), [], ["rowsT"])
            dma(rowsT[33:37, :], rows2(r_k), [], ["rowsT"])
            dma(rowsT[29:33, :], rows2(k_a), [], ["rowsT"])
            mm(pP0[:, 0:37], rowsT[0:37, :], identf[0:37, 0:37], True, True, ["rowsT", "identf"], ["pP0"])
            cp("dve", cols[:, 0:37], pP0[:, 0:37], ["pP0"], ["cols"])
            ts("dve", cols[:, 29:33], cols[:, 25:29], -1.0, 1.0, ALU.mult, ALU.add, ["cols"], ["cols"])
            dma(finw_bc[:], fin_w[0:1, :].to_broadcast([128, D]), [], ["finw_bc"])
            dma(gnw_bc[:], gn_w[0:1, :].to_broadcast([128, 512]), [], ["gnw_bc"])
            dma(gnb_bc[:], gn_b[0:1, :].to_broadcast([128, 512]), [], ["gnb_bc"])
            memset("pool", mask4[:], 1.0, ["mask4"])
            for q in range(4):
                asel(mask4[:, q, :], [[1, 128]], ALU.is_gt if q % 2 == 0 else ALU.is_ge, -1, 0, "mask4")
            memset("pool", mstT[:], 1.0, ["mstT"])
            asel(mstT[:], [[-1, 128]], ALU.is_gt, 1, 0, "mstT")
            memset("pool", blk[:], 0.0, ["blk"])
            memset("pool", blk[0:64, 0:64], 1.0, ["blk"])
            memset("pool", blk[64:128, 64:128], 1.0, ["blk"])
            memset("pool", E2[:], 0.0, ["E2"])
            memset("pool", E2[0:64, 0:1], 1.0, ["E2"])
            memset("pool", E2[64:128, 1:2], 1.0, ["E2"])
            memset("pool", PF[:], 0.0, ["PF"])
            memset("pool", STf[:], 0.0, ["STf"])
            memset("pool", STb[:], 0.0, ["STb"])

            pSQv = pSQ[:, :].rearrange("p (f l t) -> p f l t", f=2, l=4)
            pTm1 = pSQ[:, 512:1024].bitcast(BF16)
            def pre2(b):
                par = b % 2
                Tprev = blk_T(b - 1) if b > 0 else 0
                T = blk_T(b)
                slot = b % 3
                front(b, slot)
                if b > 0:
                    cp("dve", PF[:, :, 0:1], PF[:, :, Tprev:Tprev + 1], ["PF"], ["PFh"])
                for gi, (c0, n) in enumerate(((0, 4), (4, 4), (8, 4), (12, 1))):
                    pb = pPs[gi % 2]
                    pn = "pP%d" % (gi % 2)
                    pv = proj_fm(pb, pn, c0 * 128, n, T)
                    cp("act", PF[:, c0:c0 + n, 1:1 + T], pv[:, 0:n, 0:T], [pn, "PFh"], ["PF"])
                if b > 0:
                    proj_tm(pP1, "pP1", 1664, 512, T)
                    act(zr[0:T, par, :], pP1[0:T, :], AF.Silu, ["pP1"], ["zr%d" % par])
                yield
                tt("dve", Dl[:, 0:8, 0:T], PF[:, 0:8, 0:T], PF[:, 0:8, 1:1 + T], ALU.subtract, ["PF", "PFh"], ["Dl"])
                tt("pool", Dl[:, 8:13, 0:T], PF[:, 8:13, 0:T], PF[:, 8:13, 1:1 + T], ALU.subtract, ["PF", "PFh"], ["DlB"])
                for c in range(8):
                    stt(Dl[:, c, 0:T], Dl[:, c, 0:T], cols[:, c:c + 1], PF[:, c, 1:1 + T], ALU.mult, ALU.add, ["Dl", "cols", "PF"], ["Dl"])
                tt("pool", Dl[:, 8:13, 0:T], Dl[:, 8:13, 0:T], cols[:, 8:13].unsqueeze(2).to_broadcast([128, 5, T]), ALU.mult,
                   ["DlB", "cols"], ["DlB"])
                tt("pool", Dl[:, 8:13, 0:T], Dl[:, 8:13, 0:T], PF[:, 8:13, 1:1 + T], ALU.add, ["DlB", "PF"], ["DlB"])
                Rv, Kv, Vv = Dl[:, 0:4, 0:T], Dl[:, 4:8, 0:T], Dl[:, 8:12, 0:T]
                yield
                act(LR[0:64, 0:T], Dl[0:64, 12, 0:T], AF.Tanh, ["DlB"], ["LR"])
                act(LR[64:128, 0:T], Dl[64:128, 12, 0:T], AF.Copy, ["DlB"], ["LR2"])
                pv0 = pP0[:, :].rearrange("p (c t) -> p c t", c=4)
                pv1 = pP1[:, :].rearrange("p (c t) -> p c t", c=4)
                for g in range(4):
                    mm(pv0[:, g, 0:T], WA[:, 0, g * 128:(g + 1) * 128], LR[:, 0:T], True, True, ["W", "LR", "LR2"], ["pP0"])
                for g in range(4):
                    mm(pv1[:, g, 0:T], WA[:, 1, g * 128:(g + 1) * 128], LR[:, 0:T], True, True, ["W", "LR", "LR2"], ["pP1"])
                for g in range(4):
                    act(SG[:, g, 0:T], pv0[:, g, 0:T], AF.Sigmoid, ["pP0", "cols"], ["SG"], bias=cols[:, 13 + g:14 + g])
                for g in range(4):
                    act(AVt[:, g, 0:T], pv1[:, g, 0:T], AF.Sigmoid, ["pP1", "cols"], ["AVt"], bias=cols[:, 17 + g:18 + g])
                for g in range(4):
                    P.op("dve", lambda e, g=g, T=T: e.tensor_tensor_scan(out=CUM[:, g, 0:T], data0=ones_f[:, 0:T], data1=SG[:, g, 0:T],
                                                                        initial=0.0, op0=ALU.mult, op1=ALU.add),
                         ["SG", "ones_f"], ["CUM"])
                tt("pool", EXC[:, :, 0:T], CUM[:, :, 0:T], SG[:, :, 0:T], ALU.subtract, ["CUM", "SG"], ["EXC"])
                act(GM[:, :, 0:T], CUM[:, :, 0:T], AF.Exp, ["CUM"], ["GM"], scale=-LAM)
                act(GP[:, :, 0:T], CUM[:, :, 0:T], AF.Exp, ["CUM"], ["GP"], scale=LAM)
                act(GPREV[:, :, 0:T], EXC[:, :, 0:T], AF.Exp, ["EXC"], ["GPREV"], scale=-LAM)
                ts("dve", cols[:, 37:41], CUM[:, :, T - 1], -LAM, None, ALU.mult, None, ["CUM"], ["nb"])
                for g in range(4):
                    act(GHAT[:, g, 0:T], CUM[:, g, 0:T], AF.Exp, ["CUM", "nb"], ["GHAT"], scale=LAM, bias=cols[:, 37 + g:38 + g])
                yield
                bc = lambda c0: cols[:, c0:c0 + 4].unsqueeze(2).to_broadcast([128, 4, T])
                tt("dve", KKS[:, :, 0:T], Kv, bc(21), ALU.mult, ["Dl", "cols"], ["EXC"])
                tt("pool", SQ[:, :, 0:T], KKS[:, :, 0:T], KKS[:, :, 0:T], ALU.mult, ["EXC"], ["SQ"])
                pvR = pP0[:, :].rearrange("p (c t) -> p c t", c=4)
                for g in range(4):
                    mm(pvR[:, g, 0:T], blk[:, :], SQ[:, g, 0:T], True, True, ["SQ", "blk"], ["pP0"])
                act(RN[:, :, 0:T], pvR[:, :, 0:T], AF.Ln, ["pP0"], ["SQ"], bias=1e-12)
                act(RN[:, :, 0:T], RN[:, :, 0:T], AF.Exp, ["SQ"], ["SQ"], scale=-0.5)
                tt("dve", KK[:, :, 0:T], KKS[:, :, 0:T], RN[:, :, 0:T], ALU.mult, ["EXC", "SQ"], ["KK"])
                tt("pool", F1[:, :, 0:T], AVt[:, :, 0:T], bc(25), ALU.mult, ["AVt", "cols"], ["F1"])
                tt("pool", F1[:, :, 0:T], F1[:, :, 0:T], bc(29), ALU.add, ["F1", "cols"], ["F1"])
                tt("dve", KP[:, :, 0:T], Kv, F1[:, :, 0:T], ALU.mult, ["Dl", "F1"], ["KP"])
                tt("pool", BB[:, :, 0:T], KK[:, :, 0:T], AVt[:, :, 0:T], ALU.mult, ["KK", "AVt"], ["SG"])
                yield
                for pr_, (p0, p1) in enumerate(((0, 64), (64, 128))):
                    stt(AR[p0:p1, par, :, pr_, 0, 0:T], KK[p0:p1, :, 0:T], -1.0, GPREV[p0:p1, :, 0:T], ALU.mult, ALU.mult,
                        ["KK", "GPREV", "AR_init"], ["AR%d" % par])
                    tt("dve", AR[p0:p1, par, :, pr_, 1, 0:T], Dl[p0:p1, 0:4, 0:T], GM[p0:p1, :, 0:T], ALU.mult, ["Dl", "GM", "AR_init"], ["AR%d" % par])
                tt("pool", Kt[:, par, :, 0:T], KP[:, :, 0:T], GP[:, :, 0:T], ALU.mult, ["KP", "GP"], ["Kt%d" % par])
                tt("pool", Bt[:, par, :, 0:T], BB[:, :, 0:T], GP[:, :, 0:T], ALU.mult, ["SG", "GP"], ["Bt%d" % par])
                tt("dve", Kh[:, :, 0:T], KP[:, :, 0:T], GHAT[:, :, 0:T], ALU.mult, ["KP", "GHAT"], ["Kh"])
                tt("pool", Bh[:, :, 0:T], BB[:, :, 0:T], GHAT[:, :, 0:T], ALU.mult, ["SG", "GHAT"], ["Bh"])
                cp("act", vb[:, :, 0:T], Vv, ["DlB"], ["vb"])
                cp("dve", gamC[:, par, :], GM[:, :, T - 1], ["GM"], ["gamC%d" % par])
                if b > 0:
                    tt("dve", RK[:, :, 0:T], Rv, KP[:, :, 0:T], ALU.mult, ["Dl", "KP"], ["F1"])
                    tt("pool", RK[:, :, 0:T], RK[:, :, 0:T], bc(33), ALU.mult, ["F1", "cols"], ["F1"])
                    for g in range(4):
                        mm(pP1[0:T, 2 * g:2 * g + 2], RK[:, g, 0:T], E2[:, :], True, True, ["F1", "E2"], ["pP1"])
                    cp("act", sbv[0:T, par, :], pP1[0:T, 0:8], ["pP1"], ["sb%d" % par])
                yield
                pTv4 = pT[:, :].rearrange("p (k c) -> p k c", k=8)
                for k_, (src, sname, dst, dname) in enumerate(((vb, "vb", VT, "VT%d" % par), (Kh, "Kh", KhT, "KhT%d" % par), (Bh, "Bh", BhT, "BhT%d" % par))):
                    for g in range(4):
                        P.op("pe", lambda e, g=g, src=src, T=T: e.transpose(out=pTv4[0:T, g, :], in_=src[:, g, 0:T], identity=ident[:, :]),
                             [sname, "ident"], ["pT"])
                    cp("act" if k_ != 1 else "dve", dst[0:T, par, :], pT[0:T, 0:512], ["pT"], [dname])
                yield

            def main2(b):
                T = blk_T(b)
                par = b % 2
                tg = tail2(b - 1) if b >= 2 else iter(())

                def adv(k):
                    for _ in range(k):
                        if next(tg, "end") == "end":
                            return
                P.set_q(10 ** 9, 1)
                nlev = max(1, int(math.ceil(math.log2(T))))
                ARn = ["AR%d" % par, "AR_init"]
                for h in range(8):
                    g, hp = h // 2, h % 2
                    pb, pbn = (pR0, "pR0") if h % 2 == 0 else (pX, "pX")
                    pr = pb[0:T, :].rearrange("p (q t) -> p q t", q=4)
                    if T == 128:
                        arf = AR[:, par, g, hp, :, :].rearrange("p a t -> p (a t)")
                        mm(pb[0:T, 0:256], Kt[:, par, g, 0:T], arf, True, True, ["Kt%d" % par] + ARn, [pbn])
                        mm(pb[0:T, 256:512], Bt[:, par, g, 0:T], arf, True, True, ["Bt%d" % par] + ARn, [pbn])
                    else:
                        for q in range(2):
                            mm(pr[:, q, 0:T], Kt[:, par, g, 0:T], AR[:, par, g, hp, q, 0:T], True, True, ["Kt%d" % par] + ARn, [pbn])
                            mm(pr[:, 2 + q, 0:T], Bt[:, par, g, 0:T], AR[:, par, g, hp, q, 0:T], True, True, ["Bt%d" % par] + ARn, [pbn])
                    pbt, pbtn = (pSt[0:T, 256:256 + T], "pSt") if h % 2 == 0 else (pSQ[0:T, 0:T], "pSQ0")
                    mm(pbt, AR[:, par, g, hp, 0, 0:T], Bt[:, par, g, 0:T], True, True, ["Bt%d" % par] + ARn, [pbtn])
                    P.give(GV_A)
                    adv(GV_TA)
                    tt("dve", MK[0:T, h, :, 0:T], pr[:, :, 0:T], mask4[0:T, :, 0:T], ALU.mult, [pbn, "mask4"], ["MK%d" % h])
                    tt("dve", MbT[0:T, h, 0:T], pbt, mstT[0:T, 0:T], ALU.mult, [pbtn, "mstT"], ["MbT%d" % h])
                yield
                adv(10 ** 6)
                P.give(GV_X)
                for h in range(8):
                    g = h // 2
                    mm(pX[0:T, h * 64:(h + 1) * 64], AR[:, par, g, h % 2, 0, 0:T], STb[:, g, :], True, False, ARn + ["STb"], ["pX"])
                    mm(pX[0:T, h * 64:(h + 1) * 64], MK[0:T, h, 0, 0:T], VT[0:T, par, h * 64:(h + 1) * 64], False, True,
                       ["MK%d" % h, "VT%d" % par], ["pX"])
                cp("act", Xb[0:T, :, 0, :], pX[0:T, :].rearrange("p (l c) -> p l c", l=8), ["pX"], ["Xb0"])
                for lev in range(nlev):
                    yield
                    pi, po = lev % 2, (lev + 1) % 2
                    for h in range(8):
                        Pk = MK[0:T, h, 2, 0:T] if lev == 0 else PPb[0:T, h, pi, 0, 0:T]
                        pkn = "MK%d" % h if lev == 0 else "PP%d0%d" % (pi, h // 4)
                        mm(pX[0:T, h * 64:(h + 1) * 64], Pk, Xb[0:T, h, pi, :], True, False, [pkn, "Xb%d" % pi], ["pX"])
                        mm(pX[0:T, h * 64:(h + 1) * 64], ident[0:T, 0:T], Xb[0:T, h, pi, :], False, True,
                           ["ident", "Xb%d" % pi], ["pX"])
                    P.give(GV_L)
                    cp("act", Xb[0:T, :, po, :], pX[0:T, :].rearrange("p (l c) -> p l c", l=8), ["pX"], ["Xb%d" % po])
                    if lev < nlev - 1:
                        for a_ in range(2):
                            for half in range(2):
                                for l in range(4):
                                    h = half * 4 + l
                                    if lev == 0:
                                        Pk, PkT = MK[0:T, h, 2, 0:T], MbT[0:T, h, 0:T]
                                        pkn = ["MK%d" % h, "MbT%d" % h]
                                    else:
                                        Pk, PkT = PPb[0:T, h, pi, 0, 0:T], PPb[0:T, h, pi, 1, 0:T]
                                        pkn = ["PP%d0%d" % (pi, half), "PP%d1%d" % (pi, half)]
                                    if a_ == 0:
                                        mm(pSQv[0:T, half, l, 0:T], PkT, Pk, True, True, pkn, ["pSQ%d" % half])
                                    else:
                                        mm(pSQv[0:T, half, l, 0:T], Pk, PkT, True, True, pkn, ["pSQ%d" % half])
                                eng = "act" if (a_ + half) % 2 == 0 else "dve"
                                P.give(GV_S)
                                cp(eng, PPb[0:T, half * 4:half * 4 + 4, po, a_, 0:T], pSQv[0:T, half, :, 0:T],
                                   ["pSQ%d" % half], ["PP%d%d%d" % (po, a_, half)])
                fin = nlev % 2
                xfn = "Xb%d" % fin
                yield
                P.give(GV_Y)
                P.set_q(GV_T, 1)
                for h in range(8):
                    g, base = h // 2, 64 * (h % 2)
                    if b > 0:
                        yo = pR0[0:T, h * 64:(h + 1) * 64]
                        mm(yo, AR[:, par, g, h % 2, 1, 0:T], STb[:, g, :], True, False, ARn + ["STb"], ["pR0"])
                        mm(yo, MK[0:T, h, 1, 0:T], VT[0:T, par, h * 64:(h + 1) * 64], False, False, ["MK%d" % h, "VT%d" % par], ["pR0"])
                        mm(yo, MK[0:T, h, 3, 0:T], Xb[0:T, h, fin, :], False, True, ["MK%d" % h, xfn], ["pR0"])
                    so = pSt[base:base + 64, g * 64:(g + 1) * 64]
                    mm(so, KhT[0:T, par, h * 64:(h + 1) * 64], VT[0:T, par, h * 64:(h + 1) * 64], True, False,
                       ["KhT%d" % par, "VT%d" % par], ["pSt"])
                    mm(so, BhT[0:T, par, h * 64:(h + 1) * 64], Xb[0:T, h, fin, :], False, True, ["BhT%d" % par, xfn], ["pSt"])
                if b > 0:
                    cp("act", Ysb[0:T, :], pR0[0:T, :], ["pR0"], ["YS"])
                tt("dve", STf[:, :, :], STf[:, :, :], gamC[:, par, :].unsqueeze(2).to_broadcast([128, 4, 64]), ALU.mult,
                   ["STf", "gamC%d" % par], ["STf"])
                tt("dve", STf[:, :, :], STf[:, :, :], pSt[:, 0:256].rearrange("p (g c) -> p g c", g=4), ALU.add,
                   ["STf", "pSt"], ["STf"])
                cp("act", STb[:, :, :], STf[:, :, :], ["STf"], ["STb"])
                yield
                if b == NB - 1:
                    for _ in tail2(b):
                        pass

            def tail2(b):
                T = blk_T(b)
                slot = b % 3
                par = b % 2
                xr = "xt%d" % slot
                xo = xsl(slot)
                Y3 = Ysb[0:T, :].rearrange("p (h c) -> p h c", h=8)
                S3 = sqy[0:T, :].rearrange("p (h c) -> p h c", h=8)
                P.op("dve", lambda e, T=T, Y3=Y3: e.tensor_reduce(out=g8[0:T, 0:8], in_=Y3, axis=AX.X, op=ALU.add), ["YS"], ["s1"])
                act(sqy[0:T, :], Ysb[0:T, :], AF.Square, ["YS"], ["YS"])
                yield
                P.op("dve", lambda e, T=T, S3=S3: e.tensor_reduce(out=g8[0:T, 8:16], in_=S3, axis=AX.X, op=ALU.add), ["YS"], ["s2"])
                ts("dve", g8[0:T, 16:24], g8[0:T, 0:8], 1.0 / 64, None, ALU.mult, None, ["s1"], ["mean"])
                yield
                tt("dve", g8[0:T, 24:32], g8[0:T, 16:24], g8[0:T, 16:24], ALU.mult, ["mean"], ["msq"])
                stt(g8[0:T, 32:40], g8[0:T, 8:16], 1.0 / 64, g8[0:T, 24:32], ALU.mult, ALU.subtract, ["s2", "msq"], ["var"])
                yield
                act(g8[0:T, 40:48], g8[0:T, 32:40], AF.Ln, ["var"], ["gsd"], bias=GN_EPS)
                act(g8[0:T, 48:56], g8[0:T, 40:48], AF.Exp, ["gsd"], ["grs"], scale=-0.5)
                yield
                bc8 = lambda c0: g8[0:T, c0:c0 + 8].unsqueeze(2).to_broadcast([T, 8, 64])
                tt("dve", S3, Y3, bc8(16), ALU.subtract, ["YS", "mean"], ["YS"])
                yield
                tt("pool", S3, S3, bc8(48), ALU.mult, ["YS", "grs"], ["YS"])
                yield
                tt("pool", sqy[0:T, :], sqy[0:T, :], gnw_bc[0:T, :], ALU.mult, ["YS", "gnw_bc"], ["YS"])
                yield
                tt("dve", sqy[0:T, :], sqy[0:T, :], gnb_bc[0:T, :], ALU.add, ["YS", "gnb_bc"], ["YS"])
                yield
                tt("pool", Y3, VT[0:T, par, :].rearrange("p (h c) -> p h c", h=8), sbv[0:T, par, :].unsqueeze(2).to_broadcast([T, 8, 64]),
                   ALU.mult, ["VT%d" % par, "sb%d" % par, "YS"], ["YS"])
                yield
                tt("dve", sqy[0:T, :], sqy[0:T, :], Ysb[0:T, :], ALU.add, ["YS"], ["YS"])
                yield
                tt("pool", mixr[0:T, :], sqy[0:T, :], zr[0:T, par, :], ALU.mult, ["YS", "zr%d" % par], ["mixr"])
                yield
                for dc in range(8):
                    src = yfox[0:T, b - 1, dc * 128:(dc + 1) * 128] if dc < 4 else mixr[0:T, (dc - 4) * 128:(dc - 3) * 128]
                    P.op("pe", lambda e, dc=dc, src=src, T=T: e.transpose(out=pTm1[:, dc * 128:dc * 128 + T], in_=src, identity=ident[0:T, 0:T]),
                         ["mixr", "ident"], ["pSQ1"])
                    if dc % 4 == 3:
                        yield
                cp("act", mixT[:, :, 0:T], pTm1[:, :].rearrange("p (c t) -> p c t", c=8)[:, :, 0:T], ["pSQ1"], ["mixT"])
                yield
                for hf in range(2):
                    pb, pn = pSQ[:, 512:1024], "pSQ1"
                    for dc in range(8):
                        mm(pb[0:T, :], mixT[:, dc, 0:T], wout[:, dc, hf * 512:(hf + 1) * 512], dc == 0, dc == 7, ["mixT", "W"], [pn])
                    yield
                    tt("dve", Hh[0:T, hf * 512:(hf + 1) * 512], pb[0:T, :], xo[0:T, hf * 512:(hf + 1) * 512], ALU.add,
                       [pn, xr, "mixr"], ["YS"])
                    yield
                act(xs[0:T, :], Hh[0:T, :], AF.Square, ["YS"], ["xs", "ss2"], accum=st_[0:T, 4:5])
                yield
                act(st_[0:T, 5:6], st_[0:T, 4:5], AF.Ln, ["ss2"], ["sd2"], bias=NORM_EPS, scale=1.0 / D)
                act(st_[0:T, 6:7], st_[0:T, 5:6], AF.Exp, ["sd2"], ["rstd2"], scale=-0.5)
                yield
                stt(xo[0:T, :], Hh[0:T, :], st_[0:T, 6:7], finw_bc[0:T, :], ALU.mult, ALU.mult,
                    ["YS", "rstd2", "finw_bc", xr], [xr])
                dma(out[(b - 1) * 128:b * 128, :], xo[0:T, :], [xr], ["out%d" % b] + (["stg0"] if slot == 2 else []), final=True)
                yield

            if stage > 3:
                for _ in pre2(0):
                    pass
                for b in range(NB):
                    P.run_pair(drain(main2(b)), drain(pre2(b + 1) if b + 1 < NB else iter(())), Q2A, Q2B)
    P.emit()
    G.close()
    return nc


_NAMES = ["meta", "norm_w", "w_in", "b_f", "mu_shift", "w0", "w_up", "a0", "a_up", "k_k", "k_a", "r_k",
          "gn_w", "gn_b", "w_out", "final_norm_w"]


def _prep(inputs, nblk_real):
    f = lambda a: np.ascontiguousarray(np.asarray(a, dtype=np.float32))
    shared = {
        "meta": f(inputs["meta"]),
        "norm_w": f(inputs["norm_w"]).reshape(1, D),
        "w_in": f(inputs["w_in"]).reshape(D, 4232),
        "b_f": f(inputs["b_f"]).reshape(1, 8),
        "mu_shift": f(inputs["mu_shift"]).reshape(1, 1664),
        "w0": f(inputs["w0"]).reshape(1, 512),
        "w_up": f(inputs["w_up"]).reshape(64, 512),
        "a0": f(inputs["a0"]).reshape(1, 512),
        "a_up": f(inputs["a_up"]).reshape(64, 512),
        "k_k": f(inputs["k_k"]).reshape(1, 512),
        "k_a": f(inputs["k_a"]).reshape(1, 512),
        "r_k": f(inputs["r_k"]).reshape(1, 512),
        "gn_w": f(inputs["gn_w"]).reshape(1, 512),
        "gn_b": f(inputs["gn_b"]).reshape(1, 512),
        "w_out": f(inputs["w_out"]).reshape(D, D),
        "final_norm_w": f(inputs["final_norm_w"]).reshape(1, D),
    }
    xs = f(inputs["x"])
    maps = []
    for c in range(xs.shape[0]):
        m = dict(shared)
        m["x"] = np.ascontiguousarray(xs[c, :128 * nblk_real])
        maps.append(m)
    return maps


def kernel(**inputs):
    nblk_real = inputs["x"].shape[1] // 128
    nc = build(nblk_real)
    maps = _prep(inputs, nblk_real)
    ncore = len(maps)
    res = run_bass_kernel_spmd(nc, maps, core_ids=list(range(ncore)))
    return np.stack([np.asarray(r["out"], dtype=np.float32) for r in res.results], axis=0)
```

```python
import contextlib
import math
import threading
import numpy as np
import concourse.bass as bass
import concourse.mybir as mybir
from concourse.bass_utils import run_bass_kernel_spmd

F32 = mybir.dt.float32
BF16 = mybir.dt.bfloat16
AF = mybir.ActivationFunctionType
ALU = mybir.AluOpType
AX = mybir.AxisListType

ENGS = ("pe", "act", "dve", "pool", "sp")
NDMASEM = 24
LAM = math.exp(-0.5)
NORM_EPS = 1e-6
GN_EPS = 64e-5


class _Op:
    __slots__ = ("eng", "fn", "deps", "needs_inc", "val", "is_dma", "dma_slot", "dma_val", "sid")

    def __init__(self, eng, fn):
        self.eng, self.fn = eng, fn
        self.deps = []
        self.needs_inc = False
        self.val = None
        self.is_dma = False
        self.dma_slot = None
        self.dma_val = None
        self.sid = None


class Prog:
    def __init__(self, nc):
        self.nc = nc
        self.ops = {e: [] for e in ENGS}
        self.last_w = {}
        self.readers = {}
        self.ndma = 0
        self.dma_last = [None] * NDMASEM
        self.final_waits = []
        self.hook = None
        self.pe_ok = set()
        self._tl = threading.local()
        self._npair = 0

    def run_pair(self, fa, fb, qa=1, qb=1):
        cv = threading.Condition()
        st = {"turn": 0, "done": [False, False], "cnt": 0, "err": None}
        quota = [qa, qb]
        self.quota = quota
        tl = threading.local()
        self._pair = (cv, st, quota, tl)

        def hook():
            me = tl.idx
            with cv:
                st["cnt"] += 1
                if st["cnt"] >= quota[me] and not st["done"][1 - me]:
                    st["cnt"] = 0
                    st["turn"] = 1 - me
                    cv.notify_all()
                    while st["turn"] != me:
                        cv.wait()

        self._npair += 1
        npair = self._npair

        def runner(idx, f):
            tl.idx = idx
            self._tl.sid = (npair, idx)
            with cv:
                while st["turn"] != idx:
                    cv.wait()
            try:
                f()
            except BaseException as ex:
                st["err"] = ex
            finally:
                with cv:
                    st["done"][idx] = True
                    st["cnt"] = 0
                    st["turn"] = 1 - idx
                    cv.notify_all()

        self.hook = hook
        ta = threading.Thread(target=runner, args=(0, fa))
        tb = threading.Thread(target=runner, args=(1, fb))
        ta.start(); tb.start(); ta.join(); tb.join()
        self.hook = None
        self._pair = None
        if st["err"] is not None:
            raise st["err"]

    def give(self, k):
        if getattr(self, "_pair", None) is None or k <= 0:
            return
        cv, st, quota, tl = self._pair
        me = tl.idx
        with cv:
            if st["done"][1 - me]:
                return
            quota[1 - me] = k
            st["cnt"] = 0
            st["turn"] = 1 - me
            cv.notify_all()
            while st["turn"] != me:
                cv.wait()

    def set_q(self, qa, qb):
        if getattr(self, "quota", None) is not None:
            self.quota[0], self.quota[1] = qa, qb

    def _dep(self, op, d):
        if d is None or d is op:
            return
        if op.eng == "pe" and d.eng == "pe" and not d.is_dma:
            return
        if d not in op.deps:
            op.deps.append(d)
        d.needs_inc = True

    def op(self, eng, fn, reads=(), writes=(), pe_start=False):
        o = _Op(eng, fn)
        o.sid = getattr(self._tl, "sid", None)
        self.ops[eng].append(o)
        if eng == "pe" and pe_start:
            for w in writes:
                lw = self.last_w.get(w)
                if (lw is not None and lw.eng == "pe" and not lw.is_dma and not self.readers.get(w)
                        and lw.sid != getattr(self._tl, "sid", None)):
                    raise RuntimeError("PSUM resource %r: new matmul group over unread PE data" % (w,))
        for r in reads:
            self._dep(o, self.last_w.get(r))
        for w in writes:
            self._dep(o, self.last_w.get(w))
            for rd in self.readers.get(w, ()):
                self._dep(o, rd)
        for r in reads:
            self.readers.setdefault(r, []).append(o)
        for w in writes:
            self.last_w[w] = o
            self.readers[w] = []
        if self.hook is not None:
            self.hook()
        return o

    def dma(self, eng, fn, reads=(), writes=(), final=False):
        o = self.op(eng, fn, reads, writes)
        o.is_dma = True
        slot = self.ndma % NDMASEM
        o.dma_slot = slot
        o.dma_val = 16 * (self.ndma // NDMASEM + 1)
        prev = self.dma_last[slot]
        if prev is not None and prev not in o.deps:
            o.deps.append(prev)
        self.dma_last[slot] = o
        self.ndma += 1
        if final:
            self.final_waits.append(o)
        return o

    def barrier(self):
        lasts = [self.ops[e][-1] for e in ENGS if self.ops[e]]
        lasts += [d for d in self.dma_last if d is not None]
        for e in ENGS:
            o = _Op(e, None)
            self.ops[e].append(o)
            for d in lasts:
                if d is o:
                    continue
                if d not in o.deps:
                    o.deps.append(d)
                if not d.is_dma:
                    d.needs_inc = True
        self.last_w = {}
        self.readers = {}

    def emit(self):
        nc = self.nc
        with contextlib.ExitStack() as st:
            sems = {e: st.enter_context(nc.semaphore("s_" + e)) for e in ENGS}
            dsems = [st.enter_context(nc.semaphore("d_%d" % i)) for i in range(NDMASEM)]
            for e in ENGS:
                c = 0
                for o in self.ops[e]:
                    if o.is_dma or o.fn is None:
                        continue
                    if o.needs_inc:
                        c += 1
                        o.val = c
            block = st.enter_context(nc.Block())
            engobj = {"pe": "tensor", "act": "scalar", "dve": "vector", "pool": "gpsimd", "sp": "sync"}

            def make(e):
                def body(engine):
                    waited = {}
                    for o in self.ops[e]:
                        for d in o.deps:
                            if d.is_dma:
                                key, sem, v = ("d", d.dma_slot), dsems[d.dma_slot], d.dma_val
                            else:
                                key, sem, v = d.eng, sems[d.eng], d.val
                            if waited.get(key, 0) >= v:
                                continue
                            engine.wait_ge(sem, v)
                            waited[key] = v
                        if o.fn is None:
                            continue
                        ins = o.fn(engine)
                        if o.is_dma:
                            ins.then_inc(dsems[o.dma_slot], 16)
                        elif o.needs_inc:
                            ins.then_inc(sems[e], 1)
                    if e == "sp":
                        for d in self.final_waits:
                            engine.wait_ge(dsems[d.dma_slot], d.dma_val)
                return body

            for e in ENGS:
                getattr(block, engobj[e])(make(e))


D = 1024
NHEAD = 8
HD = 64
C_FOX = 2056
C_RWKV = 2176


class _Stop(Exception):
    pass


Q1A, Q1B, Q2A, Q2B = 4, 1, 2, 1
BSW = 21
GV_A, GV_X, GV_L, GV_S, GV_Y, GV_T, GV_TA = 16, 6, 3, 2, 6, 2, 1


def build(nblk_real=32, debug=False, stage=99):
    NB = nblk_real + 1
    LTOT = 16 + 128 * nblk_real
    nc = bass.Bass("TRN2", target_bir_lowering=False)
    dt_in = lambda n, s: nc.dram_tensor(n, s, F32, kind="ExternalInput").ap()
    x = dt_in("x", [128 * nblk_real, D])
    meta = dt_in("meta", [16, D])
    norm_w = dt_in("norm_w", [1, D])
    w_in = dt_in("w_in", [D, 4232])
    b_f = dt_in("b_f", [1, 8])
    mu_shift = dt_in("mu_shift", [1, 1664])
    w0 = dt_in("w0", [1, 512])
    w_up = dt_in("w_up", [64, 512])
    a0 = dt_in("a0", [1, 512])
    a_up = dt_in("a_up", [64, 512])
    k_k = dt_in("k_k", [1, 512])
    k_a = dt_in("k_a", [1, 512])
    r_k = dt_in("r_k", [1, 512])
    gn_w = dt_in("gn_w", [1, 512])
    gn_b = dt_in("gn_b", [1, 512])
    w_out = dt_in("w_out", [D, D])
    fin_w = dt_in("final_norm_w", [1, D])
    out = nc.dram_tensor("out", [128 * nblk_real, D], F32, kind="ExternalOutput").ap()
    dbg = None
    if debug:
        dbg = nc.dram_tensor("dbg", [128 * nblk_real, D], F32, kind="ExternalOutput").ap()

    P = Prog(nc)

    def mm(o, lhsT, rhs, start, stop, rd, wr):
        P.op("pe", lambda e: e.matmul(o, lhsT=lhsT, rhs=rhs, start=start, stop=stop, skip_group_check=True), rd, wr, pe_start=start)

    def act(o, i, func, rd, wr, bias=None, scale=None, accum=None):
        kw = {}
        if bias is not None:
            kw["bias"] = bias
        if scale is not None:
            kw["scale"] = scale
        if accum is not None:
            kw["accum_out"] = accum
        P.op("act", lambda e: e.activation(out=o, in_=i, func=func, **kw), rd, wr)

    def tt(eng, o, a, b, op, rd, wr):
        P.op(eng, lambda e: e.tensor_tensor(out=o, in0=a, in1=b, op=op), rd, wr)

    def ts(eng, o, a, s1, s2, op0, op1, rd, wr):
        if s2 is None:
            P.op(eng, lambda e: e.tensor_scalar(out=o, in0=a, scalar1=s1, scalar2=None, op0=op0), rd, wr)
        else:
            P.op(eng, lambda e: e.tensor_scalar(out=o, in0=a, scalar1=s1, scalar2=s2, op0=op0, op1=op1), rd, wr)

    def stt(o, a, s, b, op0, op1, rd, wr):
        P.op("dve", lambda e: e.scalar_tensor_tensor(out=o, in0=a, scalar=s, in1=b, op0=op0, op1=op1), rd, wr)

    def cp(eng, o, i, rd, wr):
        if eng == "act":
            P.op("act", lambda e: e.activation(out=o, in_=i, func=AF.Copy), rd, wr)
        else:
            P.op(eng, lambda e: e.tensor_copy(out=o, in_=i), rd, wr)

    def memset(eng, o, v, wr):
        P.op(eng, lambda e: e.memset(o, v), (), wr)

    def asel(o, pattern, cmp, cm, base, name):
        P.op("pool", lambda e: e.affine_select(out=o, in_=o, pattern=pattern, compare_op=cmp, fill=0.0,
                                               base=base, channel_multiplier=cm), [name], [name])

    def dma(o, i, rd, wr, final=False, slow=False):
        if slow:
            P.dma("sp", lambda e: e.dma_start(out=o, in_=i, allow_slow_non_contiguous=True), rd, wr, final=final)
        else:
            P.dma("sp", lambda e: e.dma_start(out=o, in_=i), rd, wr, final=final)

    def blk_T(b):
        return 16 if b == 0 else 128

    def blk_t0(b):
        return 0 if b == 0 else 16 + 128 * (b - 1)

    G = contextlib.ExitStack()
    if True:
        SB = lambda n, s, d: G.enter_context(nc.sbuf_tensor(n, s, d))
        PS = lambda n, s, d: G.enter_context(nc.psum_tensor(n, s, d))
        W = SB("W", [128, 8, C_RWKV], BF16)
        stg = SB("stg", [128, 1, 1088], F32)
        yfox = SB("yfox", [128, nblk_real, 512], BF16)
        ident = SB("ident", [128, 128], BF16)
        identf = SB("identf", [128, 128], F32)
        mincl_f = SB("mincl_f", [128, 128], F32)
        ones_f = SB("ones_f", [128, 128], F32)
        normw_col = SB("normw_col", [128, 8], F32)
        xt = SB("xt", [128, 2, D], F32)
        xs = SB("xs", [128, D], BF16)
        uT = SB("uT", [128, 8, 128], BF16)
        st_ = SB("stats", [128, 16], F32)
        pP0 = PS("pP0", [128, 512], F32)
        pT = PS("pT", [128, 1024], BF16)

        memset("pool", identf[:], 1.0, ["identf"])
        asel(identf[:], [[-1, 128]], ALU.is_equal, 1, 0, "identf")
        cp("dve", ident[:], identf[:], ["identf"], ["ident"])
        memset("pool", mincl_f[:], 1.0, ["mincl_f"])
        asel(mincl_f[:], [[1, 128]], ALU.is_ge, -1, 0, "mincl_f")
        memset("pool", ones_f[:], 1.0, ["ones_f"])
        rowsT = SB("rowsT", [48, 128], F32)
        dma(rowsT[0:8, :], norm_w[0, :].rearrange("(c p) -> c p", p=128), [], ["rowsT"])
        mm(pP0[:, 0:8], rowsT[0:8, :], identf[0:8, 0:8], True, True, ["rowsT", "identf"], ["pP0"])
        cp("dve", normw_col[:, 0:8], pP0[:, 0:8], ["pP0"], ["normw_col"])

        wq = [0]
        stg_slots = [(stg[:, 0, :], "stg0")]
        wout = SB("wout", [128, 8, D], BF16)

        def load_w(dst_fn, src_fn, ncols, nparts=128):
            i = wq[0]
            wq[0] += 1
            sap, sname = stg_slots[i % len(stg_slots)]
            dma(sap[0:nparts, 0:ncols], src_fn, ["_"], [sname])
            eng = ("act", "dve", "pool")[i % 3]
            cp(eng, dst_fn, sap[0:nparts, 0:ncols], [sname], ["W"])

        def xsl(slot):
            return xt[:, slot, :] if slot < 2 else stg[:, 0, 0:D]

        def front(b, slot):
            T = blk_T(b)
            src = meta[:, :] if b == 0 else x[(b - 1) * 128:b * 128, :]
            xr = "xt%d" % slot
            dma(xsl(slot)[0:T, :], src, ["stg0"] if slot == 2 else [], [xr, "stg0"] if slot == 2 else [xr])
            act(xs[0:T, :], xsl(slot)[0:T, :], AF.Square, [xr], ["xs", "ss"], accum=st_[0:T, 0:1])
            act(st_[0:T, 1:2], st_[0:T, 0:1], AF.Ln, ["ss"], ["sd"], bias=NORM_EPS, scale=1.0 / D)
            act(st_[0:T, 2:3], st_[0:T, 1:2], AF.Exp, ["sd"], ["rstd"], scale=-0.5)
            act(xs[0:T, :], xsl(slot)[0:T, :], AF.Copy, [xr, "rstd"], ["xs"], scale=st_[0:T, 2:3])
            for dc in range(8):
                P.op("pe", lambda e, dc=dc: e.transpose(out=pT[:, dc * 128:dc * 128 + T],
                                                        in_=xs[0:T, dc * 128:(dc + 1) * 128],
                                                        identity=ident[0:T, 0:T]), ["xs", "ident"], ["pT"])
            pTv = pT[:, :].rearrange("p (c t) -> p c t", c=8)
            tt("dve", uT[:, :, 0:T], pTv[:, :, 0:T], normw_col[:, 0:8].unsqueeze(2).to_broadcast([128, 8, T]),
               ALU.mult, ["pT", "normw_col"], ["uT"])

        def proj_fm(pb, name, col0, nch, T):
            pv = pb[:, :].rearrange("p (c t) -> p c t", c=4)
            for c in range(nch):
                for dc in range(8):
                    mm(pv[:, c, 0:T], W[:, dc, col0 + c * 128:col0 + (c + 1) * 128], uT[:, dc, 0:T],
                       dc == 0, dc == 7, ["W", "uT"], [name])
            return pv

        def proj_tm(pb, name, col0, ncol, T):
            for dc in range(8):
                mm(pb[0:T, 0:ncol], uT[:, dc, 0:T], W[:, dc, col0:col0 + ncol], dc == 0, dc == 7, ["W", "uT"], [name])

        with contextlib.ExitStack() as L1:
            S1 = lambda n, s, d: L1.enter_context(nc.sbuf_tensor(n, s, d))
            KT = S1("KT", [128, 4, LTOT], BF16)
            Vsb = S1("Vsb", [128, NB, 8, 65], BF16)
            extK = S1("extK", [128, LTOT], BF16)
            QT = S1("QT", [128, 2, 4, 2, 128], BF16)
            Rext = S1("Rext", [128, 2, 8, 128], BF16)
            hmask = S1("hmask", [48, 8, 128], BF16)
            PT = S1("PT", [128, 2, 8, 128], BF16)
            mincl_b = S1("mincl_b", [128, 128], BF16)
            ZY = S1("ZY", [128, 2, 512], F32)
            zf = ZY
            yft = S1("yft", [128, 512], F32)
            hmaskf = ZY[0:48, :, :].rearrange("p a (h t) -> p (a h) t", h=4)
            PK = S1("PK", [128, 8, 6], BF16)
            PQ = S1("PQ", [128, 8, 6], BF16)
            sm = S1("sm", [128, 64], F32)
            Aacc = S1("Aacc", [128, 8], F32)
            bf_bc = S1("bf_bc", [128, 8], F32)
            zeros_b = S1("zeros_b", [128, 512], BF16)
            stg1 = S1("stg1", [128, 2, 1088], F32)
            for k_ in range(2):
                stg_slots.append((stg1[:, k_, :], "stg%d" % (k_ + 1)))
            pS = L1.enter_context(nc.psum_tensor("pS", [128, 2048], F32))
            pO = L1.enter_context(nc.psum_tensor("pO", [128, 1024], F32))

            for dc in range(8):
                for hf in range(2):
                    load_w(W[:, dc, hf * 1028:(hf + 1) * 1028], w_in[dc * 128:(dc + 1) * 128, hf * 1028:(hf + 1) * 1028], 1028)
            cp("dve", mincl_b[:], mincl_f[:], ["mincl_f"], ["mincl_b"])
            memset("pool", hmaskf, 1.0, ["ZY"])
            asel(hmaskf, [[-6, 8], [0, 128]], ALU.is_ge, 1, 0, "ZY")
            asel(hmaskf, [[6, 8], [0, 128]], ALU.is_ge, -1, 5, "ZY")
            cp("dve", hmask[:], hmaskf, ["ZY"], ["hmask"])
            memset("pool", Vsb[:, :, :, 64:65], 1.0, ["Vones"])
            memset("pool", PK[:], 1.0, ["PK"])
            memset("pool", PQ[:], 1.0, ["PQ"])
            memset("pool", Aacc[:], 0.0, ["Aacc"])
            memset("pool", zeros_b[:], 0.0, ["zeros_b"])
            memset("pool", extK[:], 0.0, ["eK_init"])
            memset("pool", Rext[:], 0.0, ["Rext_init"])
            memset("pool", QT[:], 0.0, ["QT_init"])
            dma(bf_bc[:], b_f[0:1, :].to_broadcast([128, 8]), [], ["bf_bc"])

            pSv2 = pS[:, :].rearrange("p (s h q) -> p s h q", s=2, h=8)
            pOv = pO[:, :].rearrange("p (h c) -> p h c", h=8)
            def pre1(b):
                T = blk_T(b)
                t0 = blk_t0(b)
                par = b % 2
                two = (b - 1) <= BSW
                pPx, pPxn = (pS[:, 1536:2048], "pS11") if two else (pP0, "pP0")
                front(b, b % 2)
                yield
                if b > 0:
                    pv = proj_fm(pP0, "pP0", 0, 4, T)
                    act(QT[0:64, par, :, 0, 0:T], pv[0:64, :, 0:T], AF.Copy, ["pP0", "QT_init"], ["QT%d" % par], scale=0.125)
                    act(QT[64:128, par, :, 1, 0:T], pv[64:128, :, 0:T], AF.Copy, ["pP0", "QT_init"], ["QT%d" % par], scale=0.125)
                    yield
                pv = proj_fm(pPx, pPxn, 512, 4, T)
                cp("dve", KT[:, :, t0:t0 + T], pv[:, :, 0:T], [pPxn], ["KT%d" % b])
                yield
                proj_tm(pP0, "pP0", 1024, 512, T)
                cp("act", Vsb[0:T, b, :, 0:64], pP0[0:T, :].rearrange("p (h c) -> p h c", h=8), ["pP0"], ["V%d" % b])
                proj_tm(pPx, pPxn, 1536, 8, T)
                tt("dve", sm[0:T, 0:8], pPx[0:T, 0:8], bf_bc[0:T, :], ALU.add, [pPxn, "bf_bc"], ["xf"])
                yield
                if b > 0:
                    proj_tm(pP0, "pP0", 1544, 512, T)
                    act(zf[0:T, par, :], pP0[0:T, :], AF.Silu, ["pP0"], ["zf%d" % par, "ZY"])
                    yield
                act(sm[0:T, 8:16], sm[0:T, 0:8], AF.Exp, ["xf"], ["e1"], scale=-1.0)
                act(sm[0:T, 16:24], sm[0:T, 8:16], AF.Ln, ["e1"], ["lfn"], bias=1.0)
                mm(pPx[0:T, 16:24], mincl_f[0:T, 0:T], sm[0:T, 16:24], True, False, ["lfn", "mincl_f"], [pPxn])
                mm(pPx[0:T, 16:24], ones_f[:, 0:T], Aacc[:, :], False, True, ["Aacc", "ones_f"], [pPxn])
                cp("act", sm[0:T, 24:32], pPx[0:T, 16:24], [pPxn], ["cpos"])
                tt("dve", Aacc[0:T, :], Aacc[0:T, :], sm[0:T, 16:24], ALU.add, ["Aacc", "lfn"], ["Aacc"])
                cp("dve", PK[0:T, :, 0], sm[0:T, 24:32], ["cpos"], ["PK"])
                tt("dve", sm[0:T, 32:40], sm[0:T, 24:32], PK[0:T, :, 0], ALU.subtract, ["cpos", "PK"], ["r1"])
                cp("dve", PK[0:T, :, 1], sm[0:T, 32:40], ["r1"], ["PK"])
                tt("dve", sm[0:T, 40:48], sm[0:T, 32:40], PK[0:T, :, 1], ALU.subtract, ["r1", "PK"], ["r2"])
                cp("dve", PK[0:T, :, 2], sm[0:T, 40:48], ["r2"], ["PK"])
                yield
                P.op("pe", lambda e, T=T: e.transpose(out=pT[0:48, 0:T], in_=PK[0:T, :, :].rearrange("p h i -> p (h i)"),
                                                      identity=ident[0:T, 0:T]), ["PK", "ident"], ["pT"])
                cp("act", extK[0:48, t0:t0 + T], pT[0:48, 0:T], ["pT", "eK_init"], ["eK%d" % b])
                if b == 0:
                    return
                ts("dve", PQ[0:T, :, 3:6], PK[0:T, :, 0:3], -1.0, None, ALU.mult, None, ["PK"], ["PQ"])
                P.op("pe", lambda e, T=T: e.transpose(out=pT[0:48, 128:128 + T], in_=PQ[0:T, :, :].rearrange("p h i -> p (h i)"),
                                                      identity=ident[0:T, 0:T]), ["PQ", "ident"], ["pT"])
                tt("dve", Rext[0:48, par, :, 0:T], pT[0:48, 128:128 + T].unsqueeze(1).to_broadcast([48, 8, T]), hmask[:, :, 0:T],
                   ALU.mult, ["pT", "hmask", "Rext_init"], ["Rext%d" % par])
                yield

            def main1(b):
                if b == 0:
                    return
                T = blk_T(b)
                par = b % 2
                for hf in range(2):
                    mm(pO[0:T, hf * 512:(hf + 1) * 512], zeros_b[:, 0:T], zeros_b[:, 0:512], True, False,
                       ["zeros_b"], ["pO"])
                def s_part(kb):
                    Tk = blk_T(kb)
                    k0 = blk_t0(kb)
                    sl = kb % 2
                    ptn = "PT%d" % sl
                    ss_ = sl if b > BSW else 0
                    pSv = pSv2[:, ss_]
                    for hf in range(2):
                        psn = "pS%d%d" % (ss_, hf)
                        mm(pSv[0:Tk, hf * 4:(hf + 1) * 4, 0:T], extK[:, k0:k0 + Tk], Rext[:, par, hf * 4:(hf + 1) * 4, 0:T],
                           True, False, ["eK%d" % kb, "Rext%d" % par, "Rext_init", "eK_init"], [psn])
                        for hh in range(4):
                            h = hf * 4 + hh
                            g = h // 2
                            mm(pSv[0:Tk, h, 0:T], KT[:, g, k0:k0 + Tk], QT[:, par, g, h % 2, 0:T],
                               False, hh == 3, ["KT%d" % kb, "QT%d" % par, "QT_init"], [psn])
                        act(PT[0:Tk, sl, hf * 4:(hf + 1) * 4, 0:T], pSv[0:Tk, hf * 4:(hf + 1) * 4, 0:T], AF.Exp, [psn], [ptn])
                    if kb == b:
                        tt("dve", PT[0:Tk, sl, :, 0:T], PT[0:Tk, sl, :, 0:T],
                           mincl_b[0:Tk, 0:T].unsqueeze(1).to_broadcast([Tk, 8, T]), ALU.mult, [ptn, "mincl_b"], [ptn])

                def pv_part(kb):
                    Tk = blk_T(kb)
                    sl = kb % 2
                    ptn = "PT%d" % sl
                    for h in range(8):
                        mm(pOv[0:T, h, 0:65], PT[0:Tk, sl, h, 0:T], Vsb[0:Tk, kb, h, 0:65], False, kb == b,
                           [ptn, "V%d" % kb, "Vones"], ["pO"])

                s_part(0)
                for kb in range(b + 1):
                    if kb + 1 <= b:
                        s_part(kb + 1)
                    pv_part(kb)
                    yield
                P.op("dve", lambda e, T=T: e.reciprocal(out=sm[0:T, 48:56], in_=pOv[0:T, :, 64]), ["pO"], ["rden"])
                tt("dve", yft[0:T, :].rearrange("p (h c) -> p h c", h=8), pOv[0:T, :, 0:64],
                   sm[0:T, 48:56].unsqueeze(2).to_broadcast([T, 8, 64]), ALU.mult, ["pO", "rden"], ["yft"])
                tt("dve", yfox[0:T, b - 1, :], yft[0:T, :], zf[0:T, par, :], ALU.mult, ["yft", "zf%d" % par], ["yfox"])
                yield

            def interleave(gm, gp):
                dm = dp = False
                while not (dm and dp):
                    if not dm:
                        try:
                            next(gm)
                        except StopIteration:
                            dm = True
                    if not dp:
                        try:
                            next(gp)
                        except StopIteration:
                            dp = True

            if stage > 1:
                for _ in pre1(0):
                    pass
                drain = lambda g: (lambda: [None for _ in g])
                def load_w2():
                    for dc in range(8):
                        for hf in range(2):
                            load_w(W[:, dc, hf * 1088:(hf + 1) * 1088],
                                   w_in[dc * 128:(dc + 1) * 128, C_FOX + hf * 1088:C_FOX + (hf + 1) * 1088], 1088)
                    for dc in range(8):
                        load_w(wout[:, dc, :], w_out[dc * 128:(dc + 1) * 128, :], 1024)

                for b in range(NB):
                    if b + 1 < NB:
                        P.run_pair(drain(main1(b)), drain(pre1(b + 1)), Q1A, Q1B)
                    else:
                        P.run_pair(drain(main1(b)), load_w2, 12, 1)

        del stg_slots[1:]
        P.barrier()

        with contextlib.ExitStack() as L2:
          if stage >= 3:
            S2 = lambda n, s, d: L2.enter_context(nc.sbuf_tensor(n, s, d))
            WA = S2("WA", [128, 2, 512], BF16)
            PF = S2("PF", [128, 13, 129], F32)
            Dl = S2("Dl", [128, 13, 128], F32)
            f32t = lambda n: S2(n, [128, 4, 128], F32)
            SG, AVt, CUM, EXC = f32t("SG"), f32t("AVt"), f32t("CUM"), f32t("EXC")
            GM, GP, GPREV, GHAT = f32t("GM"), f32t("GP"), f32t("GPREV"), f32t("GHAT")
            SQ, KK, F1, KP = f32t("SQ"), f32t("KK"), f32t("F1"), f32t("KP")
            KKS, RN, RK, BB = EXC, SQ, F1, SG
            LR = S2("LR", [128, 128], BF16)
            AR = S2("AR", [128, 2, 4, 2, 2, 128], BF16)
            Kt = S2("Kt", [128, 2, 4, 128], BF16)
            Bt = S2("Bt", [128, 2, 4, 128], BF16)
            Kh = S2("Kh", [128, 4, 128], BF16)
            Bh = S2("Bh", [128, 4, 128], BF16)
            vb = S2("vb", [128, 4, 128], BF16)
            VT = S2("VT", [128, 2, 512], BF16)
            KhT = S2("KhT", [128, 2, 512], BF16)
            BhT = S2("BhT", [128, 2, 512], BF16)
            MK = S2("MK", [128, 8, 4, 128], BF16)
            MbT = S2("MbT", [128, 8, 128], BF16)
            PPb = S2("PPb", [128, 8, 2, 2, 128], BF16)
            Xb = S2("Xb", [128, 8, 2, 64], BF16)
            STf = S2("STf", [128, 4, 64], F32)
            STb = S2("STb", [128, 4, 64], BF16)
            YS = S2("YS", [128, 1024], F32)
            Ysb = YS[:, 0:512]
            sqy = YS[:, 512:1024]
            Hh = YS
            zr = S2("zr", [128, 2, 512], F32)
            sbv = S2("sbv", [128, 2, 8], F32)
            gamC = S2("gamC", [128, 2, 4], F32)
            mixr = S2("mixr", [128, 512], BF16)
            mixT = S2("mixT", [128, 8, 128], BF16)
            mask4 = S2("mask4", [128, 4, 128], F32)
            mstT = S2("mstT", [128, 128], F32)
            blk = S2("blk", [128, 128], F32)
            E2 = S2("E2", [128, 2], F32)
            finw_bc = S2("finw_bc", [128, D], F32)
            gnw_bc = S2("gnw_bc", [128, 512], F32)
            gnb_bc = S2("gnb_bc", [128, 512], F32)
            cols = S2("cols", [128, 48], F32)
            g8 = S2("g8", [128, 64], F32)
            pP1 = L2.enter_context(nc.psum_tensor("pP1", [128, 512], F32))
            pPs = [pP0, pP1]
            pR0 = L2.enter_context(nc.psum_tensor("pR0", [128, 512], F32))
            pSQ = L2.enter_context(nc.psum_tensor("pSQ", [128, 1024], F32))
            pX = L2.enter_context(nc.psum_tensor("pX", [128, 512], F32))
            pSt = L2.enter_context(nc.psum_tensor("pSt", [128, 512], F32))

            memset("pool", WA[:], 0.0, ["W"])
            memset("pool", AR[:], 0.0, ["AR_init"])
            load_w(WA[0:64, 0, :], w_up[:, :], 512, 64)
            s = 0
            dma(stg[64:128, s, 0:512], a_up[:, :], ["_"], ["stg%d" % s])
            cp("dve", WA[64:128, 1, :], stg[64:128, s, 0:512], ["stg%d" % s], ["W"])
            rows2 = lambda src: src[0, :].rearrange("(c p) -> c p", p=128)
            dma(rowsT[0:13, :], rows2(mu_shift), ["rowsT"], ["rowsT"])
            dma(rowsT[13:17, :], rows2(w0), [], ["rowsT"])
            dma(rowsT[17:21, :], rows2(a0), [], ["rowsT"])
            dma(rowsT[21:25, :], rows2(k_k), [], ["rowsT"])
            dma(rowsT[25:29, :], rows2(k_a), [], ["rowsT"])
            dma(rowsT[33:37, :], rows2(r_k), [], ["rowsT"])
            dma(rowsT[29:33, :], rows2(k_a), [], ["rowsT"])
            mm(pP0[:, 0:37], rowsT[0:37, :], identf[0:37, 0:37], True, True, ["rowsT", "identf"], ["pP0"])
            cp("dve", cols[:, 0:37], pP0[:, 0:37], ["pP0"], ["cols"])
            ts("dve", cols[:, 29:33], cols[:, 25:29], -1.0, 1.0, ALU.mult, ALU.add, ["cols"], ["cols"])
            dma(finw_bc[:], fin_w[0:1, :].to_broadcast([128, D]), [], ["finw_bc"])
            dma(gnw_bc[:], gn_w[0:1, :].to_broadcast([128, 512]), [], ["gnw_bc"])
            dma(gnb_bc[:], gn_b[0:1, :].to_broadcast([128, 512]), [], ["gnb_bc"])
            memset("pool", mask4[:], 1.0, ["mask4"])
            for q in range(4):
                asel(mask4[:, q, :], [[1, 128]], ALU.is_gt if q % 2 == 0 else ALU.is_ge, -1, 0, "mask4")
            memset("pool", mstT[:], 1.0, ["mstT"])
            asel(mstT[:], [[-1, 128]], ALU.is_gt, 1, 0, "mstT")
            memset("pool", blk[:], 0.0, ["blk"])
            memset("pool", blk[0:64, 0:64], 1.0, ["blk"])
            memset("pool", blk[64:128, 64:128], 1.0, ["blk"])
            memset("pool", E2[:], 0.0, ["E2"])
            memset("pool", E2[0:64, 0:1], 1.0, ["E2"])
            memset("pool", E2[64:128, 1:2], 1.0, ["E2"])
            memset("pool", PF[:], 0.0, ["PF"])
            memset("pool", STf[:], 0.0, ["STf"])
            memset("pool", STb[:], 0.0, ["STb"])

            pSQv = pSQ[:, :].rearrange("p (f l t) -> p f l t", f=2, l=4)
            pTm1 = pSQ[:, 512:1024].bitcast(BF16)
            def pre2(b):
                par = b % 2
                Tprev = blk_T(b - 1) if b > 0 else 0
                T = blk_T(b)
                slot = b % 3
                front(b, slot)
                if b > 0:
                    cp("dve", PF[:, :, 0:1], PF[:, :, Tprev:Tprev + 1], ["PF"], ["PFh"])
                for gi, (c0, n) in enumerate(((0, 4), (4, 4), (8, 4), (12, 1))):
                    pb = pPs[gi % 2]
                    pn = "pP%d" % (gi % 2)
                    pv = proj_fm(pb, pn, c0 * 128, n, T)
                    cp("act", PF[:, c0:c0 + n, 1:1 + T], pv[:, 0:n, 0:T], [pn, "PFh"], ["PF"])
                if b > 0:
                    proj_tm(pP1, "pP1", 1664, 512, T)
                    act(zr[0:T, par, :], pP1[0:T, :], AF.Silu, ["pP1"], ["zr%d" % par])
                yield
                tt("dve", Dl[:, 0:8, 0:T], PF[:, 0:8, 0:T], PF[:, 0:8, 1:1 + T], ALU.subtract, ["PF", "PFh"], ["Dl"])
                tt("pool", Dl[:, 8:13, 0:T], PF[:, 8:13, 0:T], PF[:, 8:13, 1:1 + T], ALU.subtract, ["PF", "PFh"], ["DlB"])
                for c in range(8):
                    stt(Dl[:, c, 0:T], Dl[:, c, 0:T], cols[:, c:c + 1], PF[:, c, 1:1 + T], ALU.mult, ALU.add, ["Dl", "cols", "PF"], ["Dl"])
                tt("pool", Dl[:, 8:13, 0:T], Dl[:, 8:13, 0:T], cols[:, 8:13].unsqueeze(2).to_broadcast([128, 5, T]), ALU.mult,
                   ["DlB", "cols"], ["DlB"])
                tt("pool", Dl[:, 8:13, 0:T], Dl[:, 8:13, 0:T], PF[:, 8:13, 1:1 + T], ALU.add, ["DlB", "PF"], ["DlB"])
                Rv, Kv, Vv = Dl[:, 0:4, 0:T], Dl[:, 4:8, 0:T], Dl[:, 8:12, 0:T]
                yield
                act(LR[0:64, 0:T], Dl[0:64, 12, 0:T], AF.Tanh, ["DlB"], ["LR"])
                act(LR[64:128, 0:T], Dl[64:128, 12, 0:T], AF.Copy, ["DlB"], ["LR2"])
                pv0 = pP0[:, :].rearrange("p (c t) -> p c t", c=4)
                pv1 = pP1[:, :].rearrange("p (c t) -> p c t", c=4)
                for g in range(4):
                    mm(pv0[:, g, 0:T], WA[:, 0, g * 128:(g + 1) * 128], LR[:, 0:T], True, True, ["W", "LR", "LR2"], ["pP0"])
                for g in range(4):
                    mm(pv1[:, g, 0:T], WA[:, 1, g * 128:(g + 1) * 128], LR[:, 0:T], True, True, ["W", "LR", "LR2"], ["pP1"])
                for g in range(4):
                    act(SG[:, g, 0:T], pv0[:, g, 0:T], AF.Sigmoid, ["pP0", "cols"], ["SG"], bias=cols[:, 13 + g:14 + g])
                for g in range(4):
                    act(AVt[:, g, 0:T], pv1[:, g, 0:T], AF.Sigmoid, ["pP1", "cols"], ["AVt"], bias=cols[:, 17 + g:18 + g])
                for g in range(4):
                    P.op("dve", lambda e, g=g, T=T: e.tensor_tensor_scan(out=CUM[:, g, 0:T], data0=ones_f[:, 0:T], data1=SG[:, g, 0:T],
                                                                        initial=0.0, op0=ALU.mult, op1=ALU.add),
                         ["SG", "ones_f"], ["CUM"])
                tt("pool", EXC[:, :, 0:T], CUM[:, :, 0:T], SG[:, :, 0:T], ALU.subtract, ["CUM", "SG"], ["EXC"])
                act(GM[:, :, 0:T], CUM[:, :, 0:T], AF.Exp, ["CUM"], ["GM"], scale=-LAM)
                act(GP[:, :, 0:T], CUM[:, :, 0:T], AF.Exp, ["CUM"], ["GP"], scale=LAM)
                act(GPREV[:, :, 0:T], EXC[:, :, 0:T], AF.Exp, ["EXC"], ["GPREV"], scale=-LAM)
                ts("dve", cols[:, 37:41], CUM[:, :, T - 1], -LAM, None, ALU.mult, None, ["CUM"], ["nb"])
                for g in range(4):
                    act(GHAT[:, g, 0:T], CUM[:, g, 0:T], AF.Exp, ["CUM", "nb"], ["GHAT"], scale=LAM, bias=cols[:, 37 + g:38 + g])
                yield
                bc = lambda c0: cols[:, c0:c0 + 4].unsqueeze(2).to_broadcast([128, 4, T])
                tt("dve", KKS[:, :, 0:T], Kv, bc(21), ALU.mult, ["Dl", "cols"], ["EXC"])
                tt("pool", SQ[:, :, 0:T], KKS[:, :, 0:T], KKS[:, :, 0:T], ALU.mult, ["EXC"], ["SQ"])
                pvR = pP0[:, :].rearrange("p (c t) -> p c t", c=4)
                for g in range(4):
                    mm(pvR[:, g, 0:T], blk[:, :], SQ[:, g, 0:T], True, True, ["SQ", "blk"], ["pP0"])
                act(RN[:, :, 0:T], pvR[:, :, 0:T], AF.Ln, ["pP0"], ["SQ"], bias=1e-12)
                act(RN[:, :, 0:T], RN[:, :, 0:T], AF.Exp, ["SQ"], ["SQ"], scale=-0.5)
                tt("dve", KK[:, :, 0:T], KKS[:, :, 0:T], RN[:, :, 0:T], ALU.mult, ["EXC", "SQ"], ["KK"])
                tt("pool", F1[:, :, 0:T], AVt[:, :, 0:T], bc(25), ALU.mult, ["AVt", "cols"], ["F1"])
                tt("pool", F1[:, :, 0:T], F1[:, :, 0:T], bc(29), ALU.add, ["F1", "cols"], ["F1"])
                tt("dve", KP[:, :, 0:T], Kv, F1[:, :, 0:T], ALU.mult, ["Dl", "F1"], ["KP"])
                tt("pool", BB[:, :, 0:T], KK[:, :, 0:T], AVt[:, :, 0:T], ALU.mult, ["KK", "AVt"], ["SG"])
                yield
                for pr_, (p0, p1) in enumerate(((0, 64), (64, 128))):
                    stt(AR[p0:p1, par, :, pr_, 0, 0:T], KK[p0:p1, :, 0:T], -1.0, GPREV[p0:p1, :, 0:T], ALU.mult, ALU.mult,
                        ["KK", "GPREV", "AR_init"], ["AR%d" % par])
                    tt("dve", AR[p0:p1, par, :, pr_, 1, 0:T], Dl[p0:p1, 0:4, 0:T], GM[p0:p1, :, 0:T], ALU.mult, ["Dl", "GM", "AR_init"], ["AR%d" % par])
                tt("pool", Kt[:, par, :, 0:T], KP[:, :, 0:T], GP[:, :, 0:T], ALU.mult, ["KP", "GP"], ["Kt%d" % par])
                tt("pool", Bt[:, par, :, 0:T], BB[:, :, 0:T], GP[:, :, 0:T], ALU.mult, ["SG", "GP"], ["Bt%d" % par])
                tt("dve", Kh[:, :, 0:T], KP[:, :, 0:T], GHAT[:, :, 0:T], ALU.mult, ["KP", "GHAT"], ["Kh"])
                tt("pool", Bh[:, :, 0:T], BB[:, :, 0:T], GHAT[:, :, 0:T], ALU.mult, ["SG", "GHAT"], ["Bh"])
                cp("act", vb[:, :, 0:T], Vv, ["DlB"], ["vb"])
                cp("dve", gamC[:, par, :], GM[:, :, T - 1], ["GM"], ["gamC%d" % par])
                if b > 0:
                    tt("dve", RK[:, :, 0:T], Rv, KP[:, :, 0:T], ALU.mult, ["Dl", "KP"], ["F1"])
                    tt("pool", RK[:, :, 0:T], RK[:, :, 0:T], bc(33), ALU.mult, ["F1", "cols"], ["F1"])
                    for g in range(4):
                        mm(pP1[0:T, 2 * g:2 * g + 2], RK[:, g, 0:T], E2[:, :], True, True, ["F1", "E2"], ["pP1"])
                    cp("act", sbv[0:T, par, :], pP1[0:T, 0:8], ["pP1"], ["sb%d" % par])
                yield
                pTv4 = pT[:, :].rearrange("p (k c) -> p k c", k=8)
                for k_, (src, sname, dst, dname) in enumerate(((vb, "vb", VT, "VT%d" % par), (Kh, "Kh", KhT, "KhT%d" % par), (Bh, "Bh", BhT, "BhT%d" % par))):
                    for g in range(4):
                        P.op("pe", lambda e, g=g, src=src, T=T: e.transpose(out=pTv4[0:T, g, :], in_=src[:, g, 0:T], identity=ident[:, :]),
                             [sname, "ident"], ["pT"])
                    cp("act" if k_ != 1 else "dve", dst[0:T, par, :], pT[0:T, 0:512], ["pT"], [dname])
                yield

            def main2(b):
                T = blk_T(b)
                par = b % 2
                tg = tail2(b - 1) if b >= 2 else iter(())

                def adv(k):
                    for _ in range(k):
                        if next(tg, "end") == "end":
                            return
                P.set_q(10 ** 9, 1)
                nlev = max(1, int(math.ceil(math.log2(T))))
                ARn = ["AR%d" % par, "AR_init"]
                for h in range(8):
                    g, hp = h // 2, h % 2
                    pb, pbn = (pR0, "pR0") if h % 2 == 0 else (pX, "pX")
                    pr = pb[0:T, :].rearrange("p (q t) -> p q t", q=4)
                    if T == 128:
                        arf = AR[:, par, g, hp, :, :].rearrange("p a t -> p (a t)")
                        mm(pb[0:T, 0:256], Kt[:, par, g, 0:T], arf, True, True, ["Kt%d" % par] + ARn, [pbn])
                        mm(pb[0:T, 256:512], Bt[:, par, g, 0:T], arf, True, True, ["Bt%d" % par] + ARn, [pbn])
                    else:
                        for q in range(2):
                            mm(pr[:, q, 0:T], Kt[:, par, g, 0:T], AR[:, par, g, hp, q, 0:T], True, True, ["Kt%d" % par] + ARn, [pbn])
                            mm(pr[:, 2 + q, 0:T], Bt[:, par, g, 0:T], AR[:, par, g, hp, q, 0:T], True, True, ["Bt%d" % par] + ARn, [pbn])
                    pbt, pbtn = (pSt[0:T, 256:256 + T], "pSt") if h % 2 == 0 else (pSQ[0:T, 0:T], "pSQ0")
                    mm(pbt, AR[:, par, g, hp, 0, 0:T], Bt[:, par, g, 0:T], True, True, ["Bt%d" % par] + ARn, [pbtn])
                    P.give(GV_A)
                    adv(GV_TA)
                    tt("dve", MK[0:T, h, :, 0:T], pr[:, :, 0:T], mask4[0:T, :, 0:T], ALU.mult, [pbn, "mask4"], ["MK%d" % h])
                    tt("dve", MbT[0:T, h, 0:T], pbt, mstT[0:T, 0:T], ALU.mult, [pbtn, "mstT"], ["MbT%d" % h])
                yield
                adv(10 ** 6)
                P.give(GV_X)
                for h in range(8):
                    g = h // 2
                    mm(pX[0:T, h * 64:(h + 1) * 64], AR[:, par, g, h % 2, 0, 0:T], STb[:, g, :], True, False, ARn + ["STb"], ["pX"])
                    mm(pX[0:T, h * 64:(h + 1) * 64], MK[0:T, h, 0, 0:T], VT[0:T, par, h * 64:(h + 1) * 64], False, True,
                       ["MK%d" % h, "VT%d" % par], ["pX"])
                cp("act", Xb[0:T, :, 0, :], pX[0:T, :].rearrange("p (l c) -> p l c", l=8), ["pX"], ["Xb0"])
                for lev in range(nlev):
                    yield
                    pi, po = lev % 2, (lev + 1) % 2
                    for h in range(8):
                        Pk = MK[0:T, h, 2, 0:T] if lev == 0 else PPb[0:T, h, pi, 0, 0:T]
                        pkn = "MK%d" % h if lev == 0 else "PP%d0%d" % (pi, h // 4)
                        mm(pX[0:T, h * 64:(h + 1) * 64], Pk, Xb[0:T, h, pi, :], True, False, [pkn, "Xb%d" % pi], ["pX"])
                        mm(pX[0:T, h * 64:(h + 1) * 64], ident[0:T, 0:T], Xb[0:T, h, pi, :], False, True,
                           ["ident", "Xb%d" % pi], ["pX"])
                    P.give(GV_L)
                    cp("act", Xb[0:T, :, po, :], pX[0:T, :].rearrange("p (l c) -> p l c", l=8), ["pX"], ["Xb%d" % po])
                    if lev < nlev - 1:
                        for a_ in range(2):
                            for half in range(2):
                                for l in range(4):
                                    h = half * 4 + l
                                    if lev == 0:
                                        Pk, PkT = MK[0:T, h, 2, 0:T], MbT[0:T, h, 0:T]
                                        pkn = ["MK%d" % h, "MbT%d" % h]
                                    else:
                                        Pk, PkT = PPb[0:T, h, pi, 0, 0:T], PPb[0:T, h, pi, 1, 0:T]
                                        pkn = ["PP%d0%d" % (pi, half), "PP%d1%d" % (pi, half)]
                                    if a_ == 0:
                                        mm(pSQv[0:T, half, l, 0:T], PkT, Pk, True, True, pkn, ["pSQ%d" % half])
                                    else:
                                        mm(pSQv[0:T, half, l, 0:T], Pk, PkT, True, True, pkn, ["pSQ%d" % half])
                                eng = "dve" if (a_ + half) % 2 == 0 else "act"
                                P.give(GV_S)
                                cp(eng, PPb[0:T, half * 4:half * 4 + 4, po, a_, 0:T], pSQv[0:T, half, :, 0:T],
                                   ["pSQ%d" % half], ["PP%d%d%d" % (po, a_, half)])
                fin = nlev % 2
                xfn = "Xb%d" % fin
                yield
                P.give(GV_Y)
                P.set_q(GV_T, 1)
                for h in range(8):
                    g, base = h // 2, 64 * (h % 2)
                    if b > 0:
                        yo = pR0[0:T, h * 64:(h + 1) * 64]
                        mm(yo, AR[:, par, g, h % 2, 1, 0:T], STb[:, g, :], True, False, ARn + ["STb"], ["pR0"])
                        mm(yo, MK[0:T, h, 1, 0:T], VT[0:T, par, h * 64:(h + 1) * 64], False, False, ["MK%d" % h, "VT%d" % par], ["pR0"])
                        mm(yo, MK[0:T, h, 3, 0:T], Xb[0:T, h, fin, :], False, True, ["MK%d" % h, xfn], ["pR0"])
                    so = pSt[base:base + 64, g * 64:(g + 1) * 64]
                    mm(so, KhT[0:T, par, h * 64:(h + 1) * 64], VT[0:T, par, h * 64:(h + 1) * 64], True, False,
                       ["KhT%d" % par, "VT%d" % par], ["pSt"])
                    mm(so, BhT[0:T, par, h * 64:(h + 1) * 64], Xb[0:T, h, fin, :], False, True, ["BhT%d" % par, xfn], ["pSt"])
                if b > 0:
                    cp("act", Ysb[0:T, :], pR0[0:T, :], ["pR0"], ["YS"])
                tt("dve", STf[:, :, :], STf[:, :, :], gamC[:, par, :].unsqueeze(2).to_broadcast([128, 4, 64]), ALU.mult,
                   ["STf", "gamC%d" % par], ["STf"])
                tt("dve", STf[:, :, :], STf[:, :, :], pSt[:, 0:256].rearrange("p (g c) -> p g c", g=4), ALU.add,
                   ["STf", "pSt"], ["STf"])
                cp("act", STb[:, :, :], STf[:, :, :], ["STf"], ["STb"])
                yield
                if b == NB - 1:
                    for _ in tail2(b):
                        pass

            def tail2(b):
                T = blk_T(b)
                slot = b % 3
                par = b % 2
                xr = "xt%d" % slot
                xo = xsl(slot)
                Y3 = Ysb[0:T, :].rearrange("p (h c) -> p h c", h=8)
                S3 = sqy[0:T, :].rearrange("p (h c) -> p h c", h=8)
                P.op("dve", lambda e, T=T, Y3=Y3: e.tensor_reduce(out=g8[0:T, 0:8], in_=Y3, axis=AX.X, op=ALU.add), ["YS"], ["s1"])
                act(sqy[0:T, :], Ysb[0:T, :], AF.Square, ["YS"], ["YS"])
                yield
                P.op("dve", lambda e, T=T, S3=S3: e.tensor_reduce(out=g8[0:T, 8:16], in_=S3, axis=AX.X, op=ALU.add), ["YS"], ["s2"])
                ts("dve", g8[0:T, 16:24], g8[0:T, 0:8], 1.0 / 64, None, ALU.mult, None, ["s1"], ["mean"])
                yield
                tt("dve", g8[0:T, 24:32], g8[0:T, 16:24], g8[0:T, 16:24], ALU.mult, ["mean"], ["msq"])
                stt(g8[0:T, 32:40], g8[0:T, 8:16], 1.0 / 64, g8[0:T, 24:32], ALU.mult, ALU.subtract, ["s2", "msq"], ["var"])
                yield
                act(g8[0:T, 40:48], g8[0:T, 32:40], AF.Ln, ["var"], ["gsd"], bias=GN_EPS)
                act(g8[0:T, 48:56], g8[0:T, 40:48], AF.Exp, ["gsd"], ["grs"], scale=-0.5)
                yield
                bc8 = lambda c0: g8[0:T, c0:c0 + 8].unsqueeze(2).to_broadcast([T, 8, 64])
                tt("dve", S3, Y3, bc8(16), ALU.subtract, ["YS", "mean"], ["YS"])
                yield
                tt("pool", S3, S3, bc8(48), ALU.mult, ["YS", "grs"], ["YS"])
                yield
                tt("pool", sqy[0:T, :], sqy[0:T, :], gnw_bc[0:T, :], ALU.mult, ["YS", "gnw_bc"], ["YS"])
                yield
                tt("dve", sqy[0:T, :], sqy[0:T, :], gnb_bc[0:T, :], ALU.add, ["YS", "gnb_bc"], ["YS"])
                yield
                tt("pool", Y3, VT[0:T, par, :].rearrange("p (h c) -> p h c", h=8), sbv[0:T, par, :].unsqueeze(2).to_broadcast([T, 8, 64]),
                   ALU.mult, ["VT%d" % par, "sb%d" % par, "YS"], ["YS"])
                yield
                tt("dve", sqy[0:T, :], sqy[0:T, :], Ysb[0:T, :], ALU.add, ["YS"], ["YS"])
                yield
                tt("pool", mixr[0:T, :], sqy[0:T, :], zr[0:T, par, :], ALU.mult, ["YS", "zr%d" % par], ["mixr"])
                yield
                for dc in range(8):
                    src = yfox[0:T, b - 1, dc * 128:(dc + 1) * 128] if dc < 4 else mixr[0:T, (dc - 4) * 128:(dc - 3) * 128]
                    P.op("pe", lambda e, dc=dc, src=src, T=T: e.transpose(out=pTm1[:, dc * 128:dc * 128 + T], in_=src, identity=ident[0:T, 0:T]),
                         ["mixr", "ident"], ["pSQ1"])
                    if dc % 4 == 3:
                        yield
                cp("act", mixT[:, :, 0:T], pTm1[:, :].rearrange("p (c t) -> p c t", c=8)[:, :, 0:T], ["pSQ1"], ["mixT"])
                yield
                for hf in range(2):
                    pb, pn = pSQ[:, 512:1024], "pSQ1"
                    for dc in range(8):
                        mm(pb[0:T, :], mixT[:, dc, 0:T], wout[:, dc, hf * 512:(hf + 1) * 512], dc == 0, dc == 7, ["mixT", "W"], [pn])
                    yield
                    tt("dve", Hh[0:T, hf * 512:(hf + 1) * 512], pb[0:T, :], xo[0:T, hf * 512:(hf + 1) * 512], ALU.add,
                       [pn, xr, "mixr"], ["YS"])
                    yield
                act(xs[0:T, :], Hh[0:T, :], AF.Square, ["YS"], ["xs", "ss2"], accum=st_[0:T, 4:5])
                yield
                act(st_[0:T, 5:6], st_[0:T, 4:5], AF.Ln, ["ss2"], ["sd2"], bias=NORM_EPS, scale=1.0 / D)
                act(st_[0:T, 6:7], st_[0:T, 5:6], AF.Exp, ["sd2"], ["rstd2"], scale=-0.5)
                yield
                stt(xo[0:T, :], Hh[0:T, :], st_[0:T, 6:7], finw_bc[0:T, :], ALU.mult, ALU.mult,
                    ["YS", "rstd2", "finw_bc", xr], [xr])
                dma(out[(b - 1) * 128:b * 128, :], xo[0:T, :], [xr], ["out%d" % b] + (["stg0"] if slot == 2 else []), final=True)
                yield

            if stage > 3:
                for _ in pre2(0):
                    pass
                for b in range(NB):
                    P.run_pair(drain(main2(b)), drain(pre2(b + 1) if b + 1 < NB else iter(())), Q2A, Q2B)
    P.emit()
    G.close()
    return nc


_NAMES = ["meta", "norm_w", "w_in", "b_f", "mu_shift", "w0", "w_up", "a0", "a_up", "k_k", "k_a", "r_k",
          "gn_w", "gn_b", "w_out", "final_norm_w"]


def _prep(inputs, nblk_real):
    f = lambda a: np.ascontiguousarray(np.asarray(a, dtype=np.float32))
    shared = {
        "meta": f(inputs["meta"]),
        "norm_w": f(inputs["norm_w"]).reshape(1, D),
        "w_in": f(inputs["w_in"]).reshape(D, 4232),
        "b_f": f(inputs["b_f"]).reshape(1, 8),
        "mu_shift": f(inputs["mu_shift"]).reshape(1, 1664),
        "w0": f(inputs["w0"]).reshape(1, 512),
        "w_up": f(inputs["w_up"]).reshape(64, 512),
        "a0": f(inputs["a0"]).reshape(1, 512),
        "a_up": f(inputs["a_up"]).reshape(64, 512),
        "k_k": f(inputs["k_k"]).reshape(1, 512),
        "k_a": f(inputs["k_a"]).reshape(1, 512),
        "r_k": f(inputs["r_k"]).reshape(1, 512),
        "gn_w": f(inputs["gn_w"]).reshape(1, 512),
        "gn_b": f(inputs["gn_b"]).reshape(1, 512),
        "w_out": f(inputs["w_out"]).reshape(D, D),
        "final_norm_w": f(inputs["final_norm_w"]).reshape(1, D),
    }
    xs = f(inputs["x"])
    maps = []
    for c in range(xs.shape[0]):
        m = dict(shared)
        m["x"] = np.ascontiguousarray(xs[c, :128 * nblk_real])
        maps.append(m)
    return maps


def kernel(**inputs):
    nblk_real = inputs["x"].shape[1] // 128
    nc = build(nblk_real)
    maps = _prep(inputs, nblk_real)
    ncore = len(maps)
    res = run_bass_kernel_spmd(nc, maps, core_ids=list(range(ncore)))
    return np.stack([np.asarray(r["out"], dtype=np.float32) for r in res.results], axis=0)
```
